# Optimizing a Trainium2 kernel written in Bass

```python
import jax, jax.numpy as jnp
from jax import lax
import numpy as np

D_MODEL = 2048
BATCH = 4
SEQ = 2048
DEPTH = 1

CHUNK = 64
N_LEFT_CHUNKS = 8
BAND_CHUNKS = N_LEFT_CHUNKS + 1
BAND = BAND_CHUNKS * CHUNK
HEAD_DIM = 128
D_MIX = D_MODEL
N_HEADS = D_MIX // HEAD_DIM
N_HEADS_A = N_HEADS // 2
N_HEADS_B = N_HEADS - N_HEADS_A
D_A = N_HEADS_A * HEAD_DIM
D_B = N_HEADS_B * HEAD_DIM
REL_CLIP = 256
N_REL = 2 * REL_CLIP + 1
Q_BLOCK = 128
D_FF = 256 * ((8 * D_MODEL // 3 + 255) // 256)
D_PLE = 256
EPS = 1e-6
NEG_INF = -1e30
D_IN = 3 * D_A + 3 * D_B + N_HEADS_B
SPLIT_POINTS = (D_A, 2 * D_A, 3 * D_A, 3 * D_A + D_B, 3 * D_A + 2 * D_B, 3 * D_A + 3 * D_B)

kernel_name = "hymba_chunked_relpos_fox_macaron"


def rms_norm(x, g):
    xf = x.astype(jnp.float32)
    y = xf * lax.rsqrt(jnp.mean(xf * xf, axis=-1, keepdims=True) + EPS)
    return (y * g.astype(jnp.float32)).astype(x.dtype)


def swiglu(x, w_gu, w_down):
    a, b = jnp.split(x @ w_gu, 2, axis=-1)
    return (jax.nn.silu(a) * b) @ w_down


def chunked_relpos_attention(q, k, v, rel_bias):
    B, S, H, Dh = q.shape
    nc = S // CHUNK
    pad = N_LEFT_CHUNKS * CHUNK
    qc = q.reshape(B, nc, CHUNK, H, Dh)
    kp = jnp.pad(k, ((0, 0), (pad, 0), (0, 0), (0, 0))).reshape(B, nc + N_LEFT_CHUNKS, CHUNK, H, Dh)
    vp = jnp.pad(v, ((0, 0), (pad, 0), (0, 0), (0, 0))).reshape(B, nc + N_LEFT_CHUNKS, CHUNK, H, Dh)
    kb = jnp.concatenate([kp[:, w:w + nc] for w in range(BAND_CHUNKS)], axis=2)
    vb = jnp.concatenate([vp[:, w:w + nc] for w in range(BAND_CHUNKS)], axis=2)
    s = jnp.einsum('bcihd,bcjhd->bchij', qc, kb).astype(jnp.float32) * (Dh ** -0.5)
    i_idx = jnp.arange(CHUNK)[:, None]
    j_idx = jnp.arange(BAND)[None, :]
    dist = jnp.clip(pad + i_idx - j_idx, -REL_CLIP, REL_CLIP) + REL_CLIP
    bias = rel_bias[:, dist].astype(jnp.float32)
    s = s + bias[None, None]
    c_idx = jnp.arange(nc)[:, None]
    w_idx = (jnp.arange(BAND) // CHUNK)[None, :]
    valid = (c_idx - N_LEFT_CHUNKS + w_idx) >= 0
    s = jnp.where(valid[None, :, None, None, :], s, NEG_INF)
    pr = jax.nn.softmax(s, axis=-1)
    o = jnp.einsum('bchij,bcjhd->bcihd', pr.astype(v.dtype), vb)
    return o.reshape(B, S, H, Dh)


def forgetting_attention(q, k, v, f_logit):
    B, S, H, Dh = q.shape
    log_f = jax.nn.log_sigmoid(f_logit.astype(jnp.float32))
    F = jnp.cumsum(log_f, axis=1)
    Ft = jnp.transpose(F, (0, 2, 1))
    scale = Dh ** -0.5
    outs = []
    for blk in range(S // Q_BLOCK):
        q0 = blk * Q_BLOCK
        q1 = q0 + Q_BLOCK
        s = jnp.einsum('bihd,bjhd->bhij', q[:, q0:q1], k[:, :q1]).astype(jnp.float32) * scale
        s = s + Ft[:, :, q0:q1, None] - Ft[:, :, None, :q1]
        mask = (q0 + jnp.arange(Q_BLOCK))[:, None] >= jnp.arange(q1)[None, :]
        s = jnp.where(mask[None, None], s, NEG_INF)
        pr = jax.nn.softmax(s, axis=-1)
        outs.append(jnp.einsum('bhij,bjhd->bihd', pr.astype(v.dtype), v[:, :q1]))
    return jnp.concatenate(outs, axis=1)


def setup_inputs(seed: int = 0) -> dict:
    key = jax.random.key(seed)
    ks = jax.random.split(key, 20)
    f32 = jnp.float32

    def nrm(k, shape, scale):
        return jax.random.normal(k, shape, f32) * scale

    def gain(k, shape):
        return 1.0 + 0.05 * jax.random.normal(k, shape, f32)

    return {
        "x": jax.random.normal(ks[0], (BATCH, SEQ, D_MODEL), f32),
        "p": jax.random.normal(ks[1], (DEPTH, BATCH, SEQ, D_PLE), f32),
        "g_ffn1": gain(ks[2], (DEPTH, D_MODEL)),
        "w_ffn1_gu": nrm(ks[3], (DEPTH, D_MODEL, 2 * D_FF), D_MODEL ** -0.5),
        "w_ffn1_down": nrm(ks[4], (DEPTH, D_FF, D_MODEL), D_FF ** -0.5),
        "g_mix": gain(ks[5], (DEPTH, D_MODEL)),
        "w_in": nrm(ks[6], (DEPTH, D_MODEL, D_IN), D_MODEL ** -0.5),
        "b_forget": 2.0 + 0.5 * jax.random.normal(ks[7], (DEPTH, N_HEADS_B), f32),
        "rel_bias": nrm(ks[8], (DEPTH, N_HEADS_A, N_REL), 0.5),
        "w_out": nrm(ks[9], (DEPTH, D_MIX, D_MODEL), D_MIX ** -0.5),
        "g_ffn2": gain(ks[10], (DEPTH, D_MODEL)),
        "w_ffn2_gu": nrm(ks[11], (DEPTH, D_MODEL, 2 * D_FF), D_MODEL ** -0.5),
        "w_ffn2_down": nrm(ks[12], (DEPTH, D_FF, D_MODEL), D_FF ** -0.5),
        "g_ple": gain(ks[13], (DEPTH, D_MODEL)),
        "w_ple_gate": nrm(ks[14], (DEPTH, D_MODEL, D_MODEL), D_MODEL ** -0.5),
        "w_ple_proj": nrm(ks[15], (DEPTH, D_PLE, D_MODEL), D_PLE ** -0.5),
        "g_final": gain(ks[16], (D_MODEL,)),
    }


def reference(x, p, g_ffn1, w_ffn1_gu, w_ffn1_down, g_mix, w_in, b_forget, rel_bias, w_out,
              g_ffn2, w_ffn2_gu, w_ffn2_down, g_ple, w_ple_gate, w_ple_proj, g_final):
    h = x
    B, S, _ = x.shape
    for i in range(DEPTH):
        h = h + 0.5 * swiglu(rms_norm(h, g_ffn1[i]), w_ffn1_gu[i], w_ffn1_down[i])
        u = rms_norm(h, g_mix[i])
        z = u @ w_in[i]
        qa, ka, va, qb, kb, vb, fl = jnp.split(z, SPLIT_POINTS, axis=-1)
        hd = lambda t, nh: t.reshape(B, S, nh, HEAD_DIM)
        o_a = chunked_relpos_attention(hd(qa, N_HEADS_A), hd(ka, N_HEADS_A), hd(va, N_HEADS_A), rel_bias[i])
        o_b = forgetting_attention(hd(qb, N_HEADS_B), hd(kb, N_HEADS_B), hd(vb, N_HEADS_B), fl + b_forget[i])
        o = jnp.concatenate([o_a.reshape(B, S, D_A), o_b.reshape(B, S, D_B)], axis=-1)
        h = h + o @ w_out[i]
        h = h + 0.5 * swiglu(rms_norm(h, g_ffn2[i]), w_ffn2_gu[i], w_ffn2_down[i])
        gate = jax.nn.sigmoid(rms_norm(h, g_ple[i]) @ w_ple_gate[i])
        h = h + gate * (p[i] @ w_ple_proj[i])
    return rms_norm(h, g_final)
```

```python
import numpy as np
from contextlib import ExitStack
import concourse.bass as bass
import concourse.mybir as mybir
from concourse.bass_utils import run_bass_kernel_spmd

F32 = mybir.dt.float32
BF16 = mybir.dt.bfloat16
AF = mybir.ActivationFunctionType
ALU = mybir.AluOpType

D = 2048
DFF = 5632
NT = 8
NTOK = NT * 128
NC_ = D // 128
NJ = DFF // 128
DIN = 6152
DPLE = 256
EPS = 1e-6
NEG = -30000.0
SCALE = 128 ** -0.5
SEM_ROLL = 24000


class T:
    __slots__ = ("name", "w", "r", "dsem", "dcount", "scoped", "epoch")

    def __init__(self, name, scoped=True):
        self.name = name
        self.w = None
        self.r = []
        self.dsem = None
        self.dcount = 0
        self.scoped = scoped
        self.epoch = 0


def PT_(name):
    return T(name, scoped=False)


class Eng:
    def __init__(self, name, h):
        self.name = name
        self.h = h
        self.sem = None
        self.count = 0
        self.waited = {}
        self.pending = False
        self.mysems = set()
        self.last = None


class K:
    def __init__(self, nc, stack):
        self.nc = nc
        self.stack = stack
        self.nsem = 0
        self.pe = Eng("pe", nc.tensor)
        self.act = Eng("act", nc.scalar)
        self.dve = Eng("dve", nc.vector)
        self.pool = Eng("pool", nc.gpsimd)
        self.sp = Eng("sp", nc.sync)
        self.engs = [self.pe, self.act, self.dve, self.pool, self.sp]
        self.owners = []
        self.ninstr = {e.name: 0 for e in self.engs}
        self.epoch = 0
        self.scope_tok = {}
        self.scope_S = []

    def new_sem(self, name):
        self.nsem += 1
        return self.stack.enter_context(self.nc.semaphore(f"{name}_{self.nsem}"))

    def _wait(self, eng, tok):
        sem, val, tile = tok
        if eng.waited.get(sem, 0) >= val:
            return
        if eng.name == "pe" and sem in eng.mysems:
            return
        if tile is not None:
            assert tile.dcount == val, f"unsafe DMA wait on {tile.name}: {val} vs {tile.dcount}"
        eng.h.wait_ge(sem, val)
        self.ninstr[eng.name] += 1
        eng.waited[sem] = val

    def _wait_all(self, eng, deps):
        best = {}
        for d in deps:
            cur = best.get(d[0])
            if cur is None or d[1] > cur[1]:
                best[d[0]] = d
        for d in best.values():
            self._wait(eng, d)

    def _scope_deps(self, ts, deps):
        touched = False
        for t in ts:
            if t.scoped:
                touched = True
                if t.epoch < self.epoch:
                    deps.extend(self.scope_S)
                    t.epoch = self.epoch
        return touched

    def _scope_note(self, tok):
        cur = self.scope_tok.get(tok[0])
        if cur is None or tok[1] > cur[1]:
            self.scope_tok[tok[0]] = tok

    def op(self, eng, fn, reads=(), writes=(), inc=True):
        deps = []
        for t in reads:
            if t.w is not None:
                deps.append(t.w)
        for t in writes:
            if t.w is not None:
                deps.append(t.w)
            deps.extend(t.r)
        touched = self._scope_deps(tuple(reads) + tuple(writes), deps)
        self._wait_all(eng, deps)
        ins = fn()
        self.ninstr[eng.name] += 1
        if eng.sem is None or (eng.count >= SEM_ROLL and not eng.pending):
            eng.sem = self.new_sem(eng.name)
            eng.mysems.add(eng.sem)
            eng.count = 0
        if inc:
            eng.count += 1
            ins.then_inc(eng.sem, 1)
            tok = (eng.sem, eng.count, None)
            eng.pending = False
        else:
            tok = (eng.sem, eng.count + 1, None)
            eng.pending = True
        eng.last = tok
        if touched:
            self._scope_note(tok)
        for t in writes:
            t.w = tok
            t.r = []
        for t in reads:
            self._addr(t, tok)
        return ins

    @staticmethod
    def _addr(t, tok):
        for i, r in enumerate(t.r):
            if r[0] is tok[0]:
                if tok[1] > r[1]:
                    t.r[i] = tok
                return
        t.r.append(tok)

    def dma(self, eng, out_ap, in_ap, dst, reads=(), owner=None, extra=(), **kw):
        deps = []
        for t in reads:
            if t.w is not None:
                deps.append(t.w)
        ow = dst if owner is None else owner
        for d_ in (dst,) + tuple(extra):
            if d_.w is not None and not (d_.w[2] is ow and not d_.r):
                deps.append(d_.w)
            deps.extend(d_.r)
        self._scope_deps(tuple(reads) + (dst,) + tuple(extra), deps)
        self._wait_all(eng, deps)
        if ow.dsem is None:
            ow.dsem = self.new_sem("d" + ow.name)
            self.owners.append(ow)
        ins = eng.h.dma_start(out=out_ap, in_=in_ap, **kw)
        self.ninstr[eng.name] += 1
        ow.dcount += 16
        ins.then_inc(ow.dsem, 16)
        tok = (ow.dsem, ow.dcount, ow)
        for d_ in (dst,) + tuple(extra):
            d_.w = tok
            d_.r = []
        for t in reads:
            self._addr(t, tok)
        return ins

    def wait_tokens(self, eng, toks):
        self._wait_all(eng, list(toks))

    def scope_switch(self):
        assert not self.pe.pending
        for e in self.engs:
            for ow in self.owners:
                if ow.dcount > 0 and ow.scoped:
                    self._wait(e, (ow.dsem, ow.dcount, ow))
        self.scope_S = list(self.scope_tok.values())
        self.epoch += 1

    def barrier(self):
        assert not self.pe.pending
        for e in self.engs:
            for o in self.engs:
                if o is not e and o.last is not None and o.name not in ("sp",) and o.sem is not None and o.count > 0:
                    self._wait(e, (o.sem, o.count, None))
            for ow in self.owners:
                if ow.dcount > 0:
                    self._wait(e, (ow.dsem, ow.dcount, ow))


def build(stage=99):
    nc = bass.Bass("TRN2", target_bir_lowering=False)

    def din(name, shape, dt=F32):
        return nc.dram_tensor(name, list(shape), dt, kind="ExternalInput")

    x_own = din("x_own", [NTOK, D]).ap()
    x_prev = din("x_prev", [NTOK, D]).ap()
    p_own = din("p_own", [NTOK, DPLE]).ap()
    w1gu = din("w1gu", [D, 2 * DFF]).ap()
    w1d = din("w1d", [DFF, D]).ap()
    w2gu = din("w2gu", [D, 2 * DFF]).ap()
    w2d = din("w2d", [DFF, D]).ap()
    w_in = din("w_in", [D, DIN]).ap()
    w_out = din("w_out", [D, D]).ap()
    w_pg = din("w_pg", [D, D]).ap()
    w_pp = din("w_pp", [DPLE, D]).ap()
    gains = din("gains", [5, 128, D]).ap()
    bfg = din("bfg", [128, 8]).ap()
    rb_toep = din("rb_toep", [8, 8, 128, 512]).ap()
    amask_d = din("amask", [8, 128, 512]).ap()
    cmask_d = din("cmask", [128, 128]).ap()
    cmat_d = din("cmat", [128, 3, 128]).ap()
    mprev_d = din("mprev", [128, 1]).ap()
    SEL8_D = din("sel8c", [128, 8, 128]).ap()
    out = nc.dram_tensor("out", [NTOK, D], F32, kind="ExternalOutput").ap()

    s_qT = nc.dram_tensor("s_qT", [16, 128, NTOK], BF16, kind="Internal").ap()
    s_kT = nc.dram_tensor("s_kT", [2, 16, 128, NTOK], BF16, kind="Internal").ap()
    s_v = nc.dram_tensor("s_v", [2, NT, 128, D], BF16, kind="Internal").ap()

    _uid = [0]

    def uq(name):
        _uid[0] += 1
        return f"{name}_u{_uid[0]}"

    with ExitStack() as st:
        k = K(nc, st)

        def sb(name, shape, dt):
            return st.enter_context(nc.sbuf_tensor(uq(name), list(shape), dt))

        H = sb("H", [128, NT, D], F32)
        tH = [PT_(f"H{t}") for t in range(NT)]
        xnT = sb("xnT", [128, NC_, NTOK], BF16)
        tX = [PT_(f"xnT{t}") for t in range(NT)]
        ident = sb("ident", [128, 128], BF16)
        identf = sb("identf", [128, 128], F32)
        t_id = PT_("ident")
        cmat = sb("cmat", [128, 3, 128], F32)
        t_cmat = PT_("cmat")
        cmask = sb("cmask", [128, 128], F32)
        t_cmask = PT_("cmask")
        mprev = sb("mprev", [128, 1], F32)
        t_mprev = PT_("mprev")
        bfgs = sb("bfgs", [128, 8], F32)
        t_bfg = PT_("bfg")
        ones_bf = sb("ones_bf", [128, 128], BF16)
        t_ones = PT_("ones")
        ss = sb("ss", [128, NT], F32)
        rstd = sb("rstd", [128, NT], F32)
        t_ss = PT_("ss")
        t_rstd = PT_("rstd")
        Lp = sb("Lp", [128, NT, 8], F32)
        Lo = sb("Lo", [128, NT, 8], F32)
        t_Lp, t_Lo = PT_("Lp"), PT_("Lo")
        NFp = sb("NFp", [128, NT, 8], F32)
        NFo = sb("NFo", [128, NT, 8], F32)
        t_NFp, t_NFo = PT_("NFp"), PT_("NFo")
        b_prev = sb("b_prev", [128, 8, 8], F32)
        t_bprev = PT_("bprev")

        pf = [st.enter_context(nc.psum_tensor(f"pf{i}", [128, 512], F32)) for i in range(8)]
        tpf = [PT_(f"pf{i}") for i in range(8)]
        pb = [pf[6 + i][:].bitcast(BF16).rearrange("p (c n) -> p c n", n=128) for i in range(2)]
        tpb = [tpf[6], tpf[7]]

        k.op(k.pool, lambda: nc.gpsimd.memset(identf[:], 1.0), writes=[t_id])
        k.op(k.pool, lambda: nc.gpsimd.affine_select(out=identf[:], in_=identf[:], pattern=[[-1, 128]],
                                                     compare_op=ALU.is_equal, fill=0.0, base=0,
                                                     channel_multiplier=1), reads=[t_id], writes=[t_id])
        k.op(k.dve, lambda: nc.vector.tensor_copy(out=ident[:], in_=identf[:]), reads=[t_id], writes=[t_id])
        k.op(k.dve, lambda: nc.vector.memset(ones_bf[:], 1.0), writes=[t_ones])
        k.dma(k.sp, cmat[:], cmat_d, t_cmat)
        k.dma(k.sp, cmask[:], cmask_d, t_cmask)
        k.dma(k.sp, mprev[:], mprev_d, t_mprev)
        k.dma(k.sp, bfgs[:], bfg, t_bfg)

        alt = [0]

        def evac_engine():
            alt[0] ^= 1
            return k.act if alt[0] else k.dve

        def copy_on(eng, out_ap, in_ap, reads, writes):
            if eng is k.act:
                return k.op(k.act, lambda: nc.scalar.copy(out=out_ap, in_=in_ap), reads=reads, writes=writes)
            return k.op(k.dve, lambda: nc.vector.tensor_copy(out=out_ap, in_=in_ap), reads=reads, writes=writes)

        EPS_AP = sb("eps_ap", [128, 1], F32)
        t_eps = PT_("eps")
        k.op(k.dve, lambda: nc.vector.memset(EPS_AP[:], EPS), writes=[t_eps])
        xn = [sb(f"xn{i}", [128, D], BF16) for i in range(2)]
        t_xn = [PT_(f"xn{i}") for i in range(2)]
        gbc = sb("gbc", [128, D], F32)
        t_g = PT_("gbc")
        t_sst = [PT_(f"ss{t}") for t in range(NT)]
        t_rst = [PT_(f"rstd{t}") for t in range(NT)]

        def load_gain(gi):
            k.dma(k.sp, gbc[:], gains[gi], t_g)

        def norm_stats_tile(t, junk_ap, junk_t):
            k.op(k.act, lambda: nc.scalar.activation(out=junk_ap, in_=H[:, t, :], func=AF.Square,
                                                     accum_out=ss[:, t:t + 1]),
                 reads=[tH[t]], writes=[junk_t, t_sst[t]])
            k.op(k.act, lambda: nc.scalar.activation(out=rstd[:, t:t + 1], in_=ss[:, t:t + 1], func=AF.Sqrt,
                                                     scale=1.0 / D, bias=EPS_AP[:, 0:1]),
                 reads=[t_sst[t], t_eps], writes=[t_rst[t]])

        def norm_recip(t):
            k.op(k.dve, lambda: nc.vector.reciprocal(out=rstd[:, t:t + 1], in_=rstd[:, t:t + 1]),
                 reads=[t_rst[t]], writes=[t_rst[t]])

        def norm_part1a(t):
            s = t % 2
            norm_stats_tile(t, xn[s][:], t_xn[s])

        def norm_part1b(t):
            s = t % 2
            norm_recip(t)
            k.op(k.dve, lambda: nc.vector.scalar_tensor_tensor(
                out=xn[s][:], in0=H[:, t, :], scalar=rstd[:, t:t + 1], in1=gbc[:],
                op0=ALU.mult, op1=ALU.mult), reads=[tH[t], t_rst[t], t_g], writes=[t_xn[s]])

        def norm_step(t):
            if t - 2 >= 0 and t - 2 < NT:
                norm_part2(t - 2)
            if t < NT:
                norm_part1a(t)
            if t - 1 >= 0 and t - 1 < NT:
                norm_part1b(t - 1)

        def norm_part2(t):
            s = t % 2
            for half in range(2):
                for c in range(8):
                    cc = half * 8 + c
                    k.op(k.pe, lambda c=c, cc=cc, half=half: nc.tensor.transpose(
                        out=pb[half][:, c, :], in_=xn[s][:, cc * 128:(cc + 1) * 128], identity=ident[:]),
                        reads=[t_xn[s], t_id], writes=[tpb[half]], inc=(c == 7))
                k.op(k.act, lambda half=half: nc.scalar.copy(
                    out=xnT[:, half * 8:(half + 1) * 8, t * 128:(t + 1) * 128], in_=pb[half]),
                    reads=[tpb[half]], writes=[tX[t]])

        def norm_all():
            for t in range(NT + 2):
                norm_step(t)

        tile_free = [None]

        def rowproj_gen(wd_slot, t_wd, act_aps, act_tiles_fn, nrow, scale, pbase, with_norm=False):
            for t in range(NT):
                for cg in range(4):
                    bank = pbase + ((t * 4 + cg) % 2)
                    for jl in range(nrow):
                        k.op(k.pe, lambda t=t, cg=cg, jl=jl, bank=bank: nc.tensor.matmul(
                            pf[bank][:, :], lhsT=act_aps[jl][:, t * 128:(t + 1) * 128],
                            rhs=wd_slot[:, jl, cg * 512:(cg + 1) * 512],
                            start=(jl == 0), stop=(jl == nrow - 1)),
                            reads=[t_wd[cg] if isinstance(t_wd, list) else t_wd] + act_tiles_fn(t), writes=[tpf[bank]],
                            inc=(jl == nrow - 1))
                    k.op(k.dve, lambda t=t, cg=cg, bank=bank: nc.vector.scalar_tensor_tensor(
                        out=H[:, t, cg * 512:(cg + 1) * 512], in0=pf[bank][:, :], scalar=scale,
                        in1=H[:, t, cg * 512:(cg + 1) * 512], op0=ALU.mult, op1=ALU.add),
                        reads=[tpf[bank], tH[t]], writes=[tH[t]])
                    if not (with_norm and cg == 3):
                        yield
                if with_norm:
                    norm_step(t)
                    if tile_free[0] is not None and t >= 1:
                        tile_free[0](t - 1)
                    yield
            if with_norm:
                norm_step(NT)
                if tile_free[0] is not None:
                    tile_free[0](NT - 1)
                norm_step(NT + 1)

        def rowproj_pass(*a, **kw):
            for _ in rowproj_gen(*a, **kw):
                pass

        T_WGU = [[[T(f"wgu{i}a0"), T(f"wgu{i}a1")], [T(f"wgu{i}b0"), T(f"wgu{i}b1")]] for i in range(2)]
        T_WD = [T(f"wd{i}") for i in range(2)]
        T_WIN = [[T(f"win{i}_{j}") for j in range(4)] for i in range(2)]

        def ffn(wgu_d, wd_d, next_gain, entry_gain=None):
            with ExitStack() as s2:
                wgu = [s2.enter_context(nc.sbuf_tensor(uq(f"wgu{i}"), [128, NC_, 512], BF16)) for i in range(2)]
                t_wgu = T_WGU
                wd = [s2.enter_context(nc.sbuf_tensor(uq(f"wd{i}"), [128, 4, D], BF16)) for i in range(2)]
                t_wd = T_WD
                aT = [s2.enter_context(nc.sbuf_tensor(uq(f"aT{i}"), [128, 4, NTOK], BF16)) for i in range(2)]
                t_aT = [T(f"aT{i}") for i in range(2)]
                sg = [s2.enter_context(nc.sbuf_tensor(uq(f"sg{i}"), [128, 512], F32)) for i in range(2)]
                t_sg = [T(f"sg{i}") for i in range(2)]

                gu_done = set()
                d_done = set()

                def load_gu(idx):
                    if idx in gu_done:
                        return
                    gu_done.add(idx)
                    s = idx % 2
                    j0 = idx * 2
                    for ab in range(2):
                        cbase = ab * DFF + j0 * 128
                        if idx == 0:
                            for jj in range(2):
                                k.dma(k.pool, wgu[s][:, :, ab * 256 + jj * 128:ab * 256 + (jj + 1) * 128],
                                      wgu_d[:, cbase + jj * 128:cbase + (jj + 1) * 128].rearrange(
                                          "(c p) n -> p c n", p=128), t_wgu[s][ab][jj])
                        else:
                            k.dma(k.pool, wgu[s][:, :, ab * 256:(ab + 1) * 256],
                                  wgu_d[:, cbase:cbase + 256].rearrange("(c p) n -> p c n", p=128),
                                  t_wgu[s][ab][0], extra=[t_wgu[s][ab][1]])

                def load_d(pi):
                    if pi in d_done:
                        return
                    d_done.add(pi)
                    s = pi % 2
                    k.dma(k.pool, wd[s][:], wd_d[pi * 512:(pi + 1) * 512, :].rearrange("(j p) n -> p j n", p=128),
                          t_wd[s])

                NP = NJ // 4
                load_gu(0)
                if entry_gain is not None:
                    k.wait_tokens(k.pool, [tH[t].w for t in range(4) if tH[t].w is not None and tH[t].w[2] is not None])
                    load_gu(1)
                    load_d(0)
                    if entry_gain >= 0:
                        load_gain(entry_gain)
                st = {"cnt": 0, "prev": None}

                def up_step(pi, jp, jj, tg):
                    asl = pi % 2
                    s = (pi * 2 + jp) % 2
                    jl = jp * 2 + jj
                    bg = (st["cnt"] % 2) * 2
                    bu = bg + 1
                    st["cnt"] += 1
                    xt = tX[tg * 4:(tg + 1) * 4]
                    for c in range(NC_):
                        k.op(k.pe, lambda c=c: nc.tensor.matmul(
                            pf[bg][:, :], lhsT=wgu[s][:, c, jj * 128:(jj + 1) * 128],
                            rhs=xnT[:, c, tg * 512:(tg + 1) * 512], start=(c == 0), stop=(c == NC_ - 1)),
                            reads=[t_wgu[s][0][jj]] + xt, writes=[tpf[bg]], inc=(c == NC_ - 1))
                    for c in range(NC_):
                        k.op(k.pe, lambda c=c: nc.tensor.matmul(
                            pf[bu][:, :], lhsT=wgu[s][:, c, 256 + jj * 128:256 + (jj + 1) * 128],
                            rhs=xnT[:, c, tg * 512:(tg + 1) * 512], start=(c == 0), stop=(c == NC_ - 1)),
                            reads=[t_wgu[s][1][jj]] + xt, writes=[tpf[bu]], inc=(c == NC_ - 1))
                    ss_ = st["cnt"] % 2
                    k.op(k.act, lambda: nc.scalar.activation(out=sg[ss_][:], in_=pf[bg][:, :], func=AF.Silu),
                         reads=[tpf[bg]], writes=[t_sg[ss_]])
                    k.op(k.dve, lambda: nc.vector.tensor_tensor(
                        out=aT[asl][:, jl, tg * 512:(tg + 1) * 512], in0=pf[bu][:, :], in1=sg[ss_][:],
                        op=ALU.mult), reads=[tpf[bu], t_sg[ss_]], writes=[t_aT[asl]])
                    if st["prev"] is not None:
                        for _ in range(4):
                            if next(st["prev"], "done") == "done":
                                st["prev"] = None
                                break

                for pi in range(NP):
                    asl = pi % 2
                    if pi == 0 and entry_gain is not None:
                        for t in range(6):
                            norm_step(t)
                        for jp in range(2):
                            for jj in range(2):
                                up_step(0, jp, jj, 0)
                        for t in range(6, NT + 2):
                            norm_step(t)
                        for jp in range(2):
                            for jj in range(2):
                                up_step(0, jp, jj, 1)
                            load_gu(jp + 2)
                    else:
                        for jp in range(2):
                            idx = pi * 2 + jp
                            if idx + 1 < NP * 2:
                                load_gu(idx + 1)
                            if jp == 0:
                                load_d(pi)
                            for jj in range(2):
                                for tg in range(2):
                                    up_step(pi, jp, jj, tg)
                    if pi == NP - 2:
                        load_gain(next_gain)
                    if st["prev"] is not None:
                        for _ in st["prev"]:
                            pass
                    st["prev"] = rowproj_gen(wd[pi % 2], t_wd[pi % 2], [aT[asl][:, jl, :] for jl in range(4)],
                                             lambda t, asl=asl: [t_aT[asl]], 4, 0.5, 4, with_norm=(pi == NP - 1))
                for _ in st["prev"]:
                    pass
                k.scope_switch()

        def inproj(which, after_first=None):
            with ExitStack() as s2:
                win = [s2.enter_context(nc.sbuf_tensor(uq(f"win{i}"), [128, NC_, 512], BF16)) for i in range(2)]
                t_win = T_WIN
                wf = s2.enter_context(nc.sbuf_tensor(uq("wf"), [128, NC_, 8], BF16))
                t_wf = T("wf")
                zst = [s2.enter_context(nc.sbuf_tensor(uq(f"zst{i}"), [128, NTOK], BF16)) for i in range(2)]
                t_zst = [T(f"zst{i}") for i in range(2)]
                vst = [s2.enter_context(nc.sbuf_tensor(uq(f"vst{i}"), [128, 512], BF16)) for i in range(4)]
                t_vst = [T(f"vst{i}") for i in range(4)]
                xf = s2.enter_context(nc.sbuf_tensor(uq("xf"), [128, 8], F32))
                t_xf = T("xf")
                L = Lo if which else Lp
                t_L = t_Lo if which else t_Lp
                t_scr = PT_("scr")
                blocks = []
                if which:
                    blocks += [("q", 0, 0), ("q", 1, 4)]
                blocks += [("k", 2, 0), ("k", 3, 4), ("v", 4, 0), ("v", 5, 4)]
                if which:
                    blocks += [("q", 6, 8), ("q", 7, 12)]
                blocks += [("k", 8, 8), ("k", 9, 12), ("v", 10, 8), ("v", 11, 12)]

                def load_blk(i):
                    kind, cb, h0 = blocks[i]
                    s = i % 2
                    if i == 0:
                        for hh in range(4):
                            k.dma(k.pool, win[s][:, :, hh * 128:(hh + 1) * 128],
                                  w_in[:, cb * 512 + hh * 128:cb * 512 + (hh + 1) * 128].rearrange(
                                      "(c p) n -> p c n", p=128), t_win[s][hh])
                    else:
                        k.dma(k.pool, win[s][:], w_in[:, cb * 512:(cb + 1) * 512].rearrange(
                            "(c p) n -> p c n", p=128), t_win[s][0], extra=t_win[s][1:])

                def emit_flogits():
                    for t in range(NT):
                        bank = 4 + (t % 2)
                        for c in range(NC_):
                            k.op(k.pe, lambda c=c, t=t, bank=bank: nc.tensor.matmul(
                                pf[bank][:, 0:8], lhsT=xnT[:, c, t * 128:(t + 1) * 128], rhs=wf[:, c, :],
                                start=(c == 0), stop=(c == NC_ - 1)),
                                reads=[t_wf, tX[t]], writes=[tpf[bank]], inc=(c == NC_ - 1))
                        k.op(k.dve, lambda t=t, bank=bank: nc.vector.tensor_tensor(
                            out=L[:, t, :], in0=pf[bank][:, 0:8], in1=bfgs[:], op=ALU.add),
                            reads=[tpf[bank], t_bfg], writes=[t_L])
                    k.op(k.act, lambda: nc.scalar.activation(out=L[:], in_=L[:], func=AF.Exp, scale=-1.0),
                         reads=[t_L], writes=[t_L])
                    k.op(k.act, lambda: nc.scalar.activation(out=L[:], in_=L[:], func=AF.Ln, bias=ONE_AP[:, 0:1]),
                         reads=[t_L, t_one], writes=[t_L])
                def emit_cumsum():
                    NF = NFo if which else NFp
                    t_NF = t_NFo if which else t_NFp
                    for t in range(NT):
                        bank = 4 + (t % 2)
                        for tp in range(t + 1):
                            mi = 1 if tp == t else 0
                            k.op(k.pe, lambda t=t, tp=tp, mi=mi, bank=bank: nc.tensor.matmul(
                                pf[bank][:, 0:8], lhsT=cmat[:, mi, :], rhs=L[:, tp, :], start=(tp == 0), stop=(tp == t)),
                                reads=[t_cmat, t_L], writes=[tpf[bank]], inc=(tp == t))
                        k.op(k.dve, lambda t=t, bank=bank: nc.vector.tensor_copy(out=NF[:, t, :], in_=pf[bank][:, 0:8]),
                             reads=[tpf[bank]], writes=[t_NF])

                load_blk(0)
                zc = 0
                vc = 0
                bc = 0
                for i, (kind, cb, h0) in enumerate(blocks):
                    if i + 1 < len(blocks):
                        load_blk(i + 1)
                    if i == 1:
                        with nc.allow_non_contiguous_dma(reason="tiny forget-gate weight columns"):
                            k.dma(k.pool, wf[:], w_in[:, 6144:6152].rearrange("(c p) n -> p c n", p=128), t_wf)
                        if after_first is not None:
                            after_first()
                    if i == 2:
                        emit_flogits()
                    if i == 4:
                        emit_cumsum()
                    s = i % 2
                    if kind in ("q", "k"):
                        for hh in range(4):
                            h = h0 + hh
                            zs = zc % 2
                            zc += 1
                            tgs = [1] if (which == 0 and h < 8) else [0, 1]
                            for tg in tgs:
                                bank = bc % 4
                                bc += 1
                                for c in range(NC_):
                                    k.op(k.pe, lambda c=c, s=s, hh=hh, tg=tg, bank=bank: nc.tensor.matmul(
                                        pf[bank][:, :], lhsT=win[s][:, c, hh * 128:(hh + 1) * 128],
                                        rhs=xnT[:, c, tg * 512:(tg + 1) * 512], start=(c == 0), stop=(c == NC_ - 1)),
                                        reads=[t_win[s][hh]] + tX[tg * 4:(tg + 1) * 4], writes=[tpf[bank]],
                                        inc=(c == NC_ - 1))
                                copy_on(evac_engine(), zst[zs][:, tg * 512:(tg + 1) * 512], pf[bank][:, :],
                                        [tpf[bank]], [t_zst[zs]])
                            dst = s_qT[h] if kind == "q" else s_kT[which, h]
                            if len(tgs) == 1:
                                k.dma(k.sp, dst[:, 512:1024], zst[zs][:, 512:1024], PT_("scr"), reads=[t_zst[zs]],
                                      owner=t_zst[zs])
                            else:
                                k.dma(k.sp, dst, zst[zs][:], PT_("scr"), reads=[t_zst[zs]], owner=t_zst[zs])
                    else:
                        for t in (range(4, NT) if (which == 0 and h0 < 8) else range(NT)):
                            bank = bc % 4
                            bc += 1
                            vs = vc % 4
                            vc += 1
                            for c in range(NC_):
                                k.op(k.pe, lambda c=c, s=s, t=t, bank=bank: nc.tensor.matmul(
                                    pf[bank][:, :], lhsT=xnT[:, c, t * 128:(t + 1) * 128], rhs=win[s][:, c, :],
                                    start=(c == 0), stop=(c == NC_ - 1)),
                                    reads=t_win[s] + [tX[t]], writes=[tpf[bank]], inc=(c == NC_ - 1))
                            copy_on(evac_engine(), vst[vs][:], pf[bank][:, :], [tpf[bank]], [t_vst[vs]])
                            k.dma(k.sp, s_v[which, t, :, h0 * 128:h0 * 128 + 512], vst[vs][:], PT_("scr"),
                                  reads=[t_vst[vs]], owner=t_vst[vs])
                k.scope_switch()

        ONE_AP = sb("one_ap", [128, 1], F32)
        t_one = PT_("one")
        k.op(k.dve, lambda: nc.vector.memset(ONE_AP[:], 1.0), writes=[t_one])

        def load_H(src):
            for t in range(NT):
                k.dma(k.sp, H[:, t, :], src[t * 128:(t + 1) * 128, :], tH[t])

        load_gain(0)
        load_H(x_prev)
        tile_free[0] = lambda t: k.dma(k.sp, H[:, t, :], x_own[t * 128:(t + 1) * 128, :], tH[t])
        ffn(w1gu, w1d, 1, entry_gain=-1)
        tile_free[0] = None
        inproj(0)

        ffn(w1gu, w1d, 1, entry_gain=0)
        inproj(1)

        pT = sb("pT", [128, 2, NTOK], BF16)
        t_pT = PT_("pT")
        with ExitStack() as fx:
            pall = fx.enter_context(nc.sbuf_tensor(uq("pall"), [128, NT, DPLE], BF16))
            t_pall = T("pall")
            k.dma(k.pool, pall[:], p_own.rearrange("(t p) d -> p t d", p=128), t_pall)
            for t0 in range(0, NT, 4):
                for c in range(2):
                    for tt in range(4):
                        k.op(k.pe, lambda c=c, tt=tt, t0=t0: nc.tensor.transpose(
                            out=pb[1][:, c * 4 + tt, :], in_=pall[:, t0 + tt, c * 128:(c + 1) * 128],
                            identity=ident[:]),
                            reads=[t_pall, t_id], writes=[tpb[1]], inc=(c == 1 and tt == 3))
                for c in range(2):
                    k.op(k.act, lambda c=c, t0=t0: nc.scalar.copy(
                        out=pT[:, c, t0 * 128:(t0 + 4) * 128],
                        in_=pb[1][:, c * 4:(c + 1) * 4, :]), reads=[tpb[1]], writes=[t_pT])
            sel8_d = SEL8_D
            sel8 = fx.enter_context(nc.sbuf_tensor(uq("sel8"), [128, 8, 128], BF16))
            t_sel8 = T("sel8")
            nfrow = fx.enter_context(nc.sbuf_tensor(uq("nfrow"), [128, NTOK], BF16))
            t_nfrow = T("nfrow")
            cmask_bf = fx.enter_context(nc.sbuf_tensor(uq("cmask_bf"), [128, 128], BF16))
            t_cmbf = T("cmask_bf")

            oT = xnT
            with ExitStack() as s2:
                def sbt(name, shape, dt):
                    return s2.enter_context(nc.sbuf_tensor(uq(name), list(shape), dt))
                qT = [sbt(f"qT{i}", [128, NTOK], BF16) for i in range(2)]
                kTo = [sbt(f"kTo{i}", [128, NTOK], BF16) for i in range(2)]
                kTp = [sbt(f"kTp{i}", [128, NTOK], BF16) for i in range(2)]
                vo = [sbt(f"vo{i}", [128, NT, 128], BF16) for i in range(2)]
                vp = [sbt(f"vp{i}", [128, NT, 128], BF16) for i in range(2)]
                t_hd = [[T(f"hd{i}_{j}") for j in range(5)] for i in range(2)]
                bmf = sbt("bmf", [128, 8, 512], F32)
                t_bmf = [T(f"bmf{j}") for j in range(8)]
                bmb = [sbt(f"bmb{i}", [128, 8, 512], BF16) for i in range(2)]
                t_bmb = [T(f"bmb{i}") for i in range(2)]
                amask = sbt("amask_s", [128, 8, 512], F32)
                t_am = T("amask")
                NPT = 4
                PT = [sbt(f"PT{i}", [128, 512], BF16) for i in range(NPT)]
                t_PT = [T(f"PT{i}") for i in range(NPT)]
                NST = 3
                rden = [sbt(f"rden{i}", [128, 512], F32) for i in range(2)]
                t_rden = [T(f"rden{i}") for i in range(2)]
                zero_ap = sbt("zero_ap", [128, 1], F32)
                t_zero = T("zero")
                k.op(k.dve, lambda: nc.vector.memset(zero_ap[:], 0.0), writes=[t_zero])
                for di in range(8):
                    k.dma(k.sp, amask[:, di, :], amask_d[di], t_am)

                def load_head(h, s):
                    with nc.allow_non_contiguous_dma(reason="per-head v slices"):
                        if h < 8:
                            for di in range(8):
                                k.dma(k.sp, bmf[:, di, :], rb_toep[h, di], t_bmf[di])
                        k.dma(k.sp, qT[s][:], s_qT[h], t_hd[s][0])
                        k.dma(k.sp, kTo[s][:], s_kT[1, h], t_hd[s][1])
                        if h < 8:
                            k.dma(k.sp, kTp[s][:, 512:1024], s_kT[0, h][:, 512:1024], t_hd[s][2])
                        else:
                            k.dma(k.sp, kTp[s][:], s_kT[0, h], t_hd[s][2])
                        k.dma(k.sp, vo[s][:], s_v[1, :, :, h * 128:(h + 1) * 128].rearrange("t p d -> p t d"),
                              t_hd[s][3])
                        if h < 8:
                            k.dma(k.sp, vp[s][:, 4:8, :],
                                  s_v[0, 4:8, :, h * 128:(h + 1) * 128].rearrange("t p d -> p t d"), t_hd[s][4])
                        else:
                            k.dma(k.sp, vp[s][:], s_v[0, :, :, h * 128:(h + 1) * 128].rearrange("t p d -> p t d"),
                                  t_hd[s][4])

                NSC = 4
                units = []
                for hi in range(16):
                    h = hi
                    s = hi % 2
                    for qg in range(2):
                        Q0 = qg * 512
                        grp = []
                        if h < 8:
                            for di in range(8):
                                K0 = Q0 - 512 + 128 * di
                                if K0 < 0:
                                    grp.append(dict(kind="A", kt=kTp[s], vt=vp[s], kb=(NTOK + K0) // 128, c0=0, di=di,
                                                    prev=True))
                                else:
                                    grp.append(dict(kind="A", kt=kTo[s], vt=vo[s], kb=K0 // 128, c0=0, di=di,
                                                    prev=False))
                        else:
                            for kb in range(8):
                                grp.append(dict(kind="B", kt=kTp[s], vt=vp[s], kb=kb, c0=0, di=None, prev=True))
                            for kb in range(4 * qg + 4):
                                grp.append(dict(kind="B", kt=kTo[s], vt=vo[s], kb=kb, c0=max(0, kb - 4 * qg), di=None,
                                                prev=False, diag=(kb >= 4 * qg)))
                        for gi_, u in enumerate(grp):
                            u.update(h=h, s=s, qg=qg, Q0=Q0, first=(gi_ == 0), last=(gi_ == len(grp) - 1),
                                     gidx=hi * 2 + qg, newhead=(qg == 0 and gi_ == 0), hi=hi)
                            units.append(u)
                for i_, u in enumerate(units):
                    u["sc"] = i_ % NSC
                    u["pt"] = i_ % NPT
                    u["st"] = i_ % NST

                def emit_qk(u):
                    s, c0, Q0, sc = u["s"], u["c0"], u["Q0"], u["sc"]
                    kb = u["kb"]
                    ncols = slice(c0 * 128, 512)
                    extra = []
                    if u["kind"] == "A":
                        extra.append("bias")
                    if u["kind"] == "B":
                        extra.append("aug")
                        if u.get("diag"):
                            extra.append("mask")
                    k.op(k.pe, lambda: nc.tensor.matmul(
                        pf[sc][:, ncols], lhsT=u["kt"][:, kb * 128:(kb + 1) * 128],
                        rhs=qT[s][:, Q0 + c0 * 128:Q0 + 512], start=True, stop=(not extra)),
                        reads=t_hd[s], writes=[tpf[sc]], inc=(not extra))
                    if "bias" in extra:
                        di = u["di"]
                        k.op(k.pe, lambda: nc.tensor.matmul(
                            pf[sc][:, :], lhsT=ident[:], rhs=bmb[s][:, di, :], start=False, stop=True),
                            reads=[t_id, t_bmb[s]], writes=[tpf[sc]], inc=True)
                    if "aug" in extra:
                        hb = u["h"] - 8
                        lastx = ("mask" not in extra)
                        k.op(k.pe, lambda: nc.tensor.matmul(
                            pf[sc][:, ncols], lhsT=sel8[:, hb, :], rhs=nfrow[:, Q0 + c0 * 128:Q0 + 512],
                            start=False, stop=lastx), reads=[t_sel8, t_nfrow], writes=[tpf[sc]], inc=lastx)
                    if "mask" in extra:
                        k.op(k.pe, lambda: nc.tensor.matmul(
                            pf[sc][:, c0 * 128:(c0 + 1) * 128], lhsT=ident[:], rhs=cmask_bf[:],
                            start=False, stop=True), reads=[t_id, t_cmbf], writes=[tpf[sc]], inc=True)

                def emit_soft(u):
                    s, c0, sc, pt = u["s"], u["c0"], u["sc"], u["pt"]
                    kb = u["kb"]
                    if u["kind"] == "A":
                        bias_ap = mprev[:, 0:1] if u["prev"] else zero_ap[:, 0:1]
                        k.op(k.act, lambda: nc.scalar.activation(out=PT[pt][:], in_=pf[sc][:, :], func=AF.Exp,
                                                                 scale=SCALE, bias=bias_ap),
                             reads=[tpf[sc], t_mprev, t_zero], writes=[t_PT[pt]])
                    else:
                        hb = u["h"] - 8
                        bias_ap = b_prev[:, kb, hb:hb + 1] if u["prev"] else NFo[:, kb, hb:hb + 1]
                        k.op(k.act, lambda: nc.scalar.activation(
                            out=PT[pt][:, c0 * 128:512], in_=pf[sc][:, c0 * 128:512], func=AF.Exp, scale=SCALE,
                            bias=bias_ap), reads=[tpf[sc], t_bprev, t_NFo], writes=[t_PT[pt]])

                def emit_pv(u):
                    s, c0, pt = u["s"], u["c0"], u["pt"]
                    po = 4 + (u["gidx"] % 2) * 2
                    pd_ = po + 1
                    k.op(k.pe, lambda: nc.tensor.matmul(
                        pf[po][:, c0 * 128:512], lhsT=u["vt"][:, u["kb"], :], rhs=PT[pt][:, c0 * 128:512],
                        start=u["first"], stop=u["last"]), reads=t_hd[s] + [t_PT[pt]], writes=[tpf[po]], inc=False)
                    k.op(k.pe, lambda: nc.tensor.matmul(
                        pf[pd_][:, c0 * 128:512], lhsT=ones_bf[:], rhs=PT[pt][:, c0 * 128:512],
                        start=u["first"], stop=u["last"]), reads=[t_ones, t_PT[pt]], writes=[tpf[pd_]], inc=True)
                    if u["last"]:
                        r_ = u["gidx"] % 2
                        h, Q0, qg = u["h"], u["Q0"], u["qg"]
                        k.op(k.dve, lambda: nc.vector.reciprocal(out=rden[r_][:], in_=pf[pd_][:, :]),
                             reads=[tpf[pd_]], writes=[t_rden[r_]])
                        k.op(k.dve, lambda: nc.vector.tensor_tensor(
                            out=oT[:, h, Q0:Q0 + 512], in0=pf[po][:, :], in1=rden[r_][:], op=ALU.mult),
                            reads=[tpf[po], t_rden[r_]], writes=tX[qg * 4:(qg + 1) * 4])

                LA = 3
                loaded = set()

                def ensure_head(hi):
                    if hi < 16 and hi not in loaded:
                        loaded.add(hi)
                        load_head(hi, hi % 2)
                        if hi < 8:
                            s_ = hi % 2
                            k.op(k.dve, lambda: nc.vector.scalar_tensor_tensor(
                                out=bmb[s_][:], in0=bmf[:], scalar=1.0 / SCALE, in1=amask[:], op0=ALU.mult, op1=ALU.add),
                                reads=t_bmf + [t_am], writes=[t_bmb[s_]])

                ensure_head(0)
                ntp = s2.enter_context(nc.sbuf_tensor(uq("ntp"), [128, 8], F32))
                nfb = s2.enter_context(nc.sbuf_tensor(uq("nfb"), [128, NT, 8], BF16))
                t_ntp, t_nfb = T("ntp"), T("nfb")
                k.dma(k.pool, sel8[:], sel8_d, t_sel8)
                k.op(k.dve, lambda: nc.vector.tensor_copy(out=cmask_bf[:], in_=cmask[:]), reads=[t_cmask],
                     writes=[t_cmbf])
                k.op(k.dve, lambda: nc.vector.memset(nfrow[:], 0.0), writes=[t_nfrow])
                k.op(k.dve, lambda: nc.vector.tensor_scalar(out=nfb[:], in0=NFo[:], scalar1=-1.0 / SCALE, scalar2=None,
                                                            op0=ALU.mult), reads=[t_NFo], writes=[t_nfb])
                for t in range(NT):
                    k.op(k.pe, lambda t=t: nc.tensor.transpose(out=pb[0][0:8, t, :], in_=nfb[:, t, :], identity=ident[:]),
                         reads=[t_nfb, t_id], writes=[tpb[0]], inc=(t == NT - 1))
                k.op(k.act, lambda: nc.scalar.copy(out=nfrow[0:8, :].rearrange("p (t n) -> p t n", n=128),
                                                   in_=pb[0][0:8, :, :]), reads=[tpb[0]], writes=[t_nfrow])
                for t in range(NT):
                    k.op(k.pe, lambda t=t: nc.tensor.matmul(pf[1][:, 0:8], lhsT=cmat[:, 0, :], rhs=Lp[:, t, :],
                                                            start=(t == 0), stop=(t == NT - 1)),
                         reads=[t_cmat, t_Lp], writes=[tpf[1]], inc=(t == NT - 1))
                k.op(k.dve, lambda: nc.vector.tensor_copy(out=ntp[:], in_=pf[1][:, 0:8]), reads=[tpf[1]], writes=[t_ntp])
                k.op(k.dve, lambda: nc.vector.tensor_scalar(out=ntp[:], in0=ntp[:], scalar1=mprev[:, 0:1], scalar2=None,
                                                            op0=ALU.subtract), reads=[t_ntp, t_mprev], writes=[t_ntp])
                for kb in range(8):
                    k.op(k.dve, lambda kb=kb: nc.vector.tensor_tensor(out=b_prev[:, kb, :], in0=NFp[:, kb, :], in1=ntp[:],
                                                                      op=ALU.subtract),
                         reads=[t_NFp, t_ntp], writes=[t_bprev])
                n = len(units)
                for i_ in range(min(LA, n)):
                    emit_qk(units[i_])
                for i_ in range(n):
                    u = units[i_]
                    if u["newhead"]:
                        ensure_head(u["hi"] + 1)
                    emit_soft(u)
                    if i_ + LA < n:
                        emit_qk(units[i_ + LA])
                    emit_pv(u)
                k.scope_switch()

        with ExitStack() as s2:
            wo = [s2.enter_context(nc.sbuf_tensor(uq(f"wo{i}"), [128, 4, D], BF16)) for i in range(2)]
            t_wo = [[T(f"wo{i}_{c}") for c in range(4)] for i in range(2)]

            def load_o(pi):
                src = w_out[pi * 512:(pi + 1) * 512, :].rearrange("(j p) n -> p j n", p=128)
                if pi == 0:
                    for cg in range(4):
                        k.dma(k.pool, wo[0][:, :, cg * 512:(cg + 1) * 512], src[:, :, cg * 512:(cg + 1) * 512],
                              t_wo[0][cg])
                else:
                    k.dma(k.pool, wo[pi % 2][:], src, t_wo[pi % 2][0], extra=t_wo[pi % 2][1:])

            load_o(0)
            load_gain(2)
            for pi in range(4):
                if pi + 1 < 4:
                    load_o(pi + 1)
                rowproj_pass(wo[pi % 2], t_wo[pi % 2], [oT[:, pi * 4 + jl, :] for jl in range(4)],
                             lambda t: [tX[t]], 4, 1.0, 4, with_norm=(pi == 3))
            k.scope_switch()

        ffn(w2gu, w2d, 3)

        with ExitStack() as s2:
            wg = [s2.enter_context(nc.sbuf_tensor(uq(f"wg{i}"), [128, NC_, 512], BF16)) for i in range(2)]
            t_wg = [[T(f"wg{i}_{q}") for q in range(4)] for i in range(2)]
            wp = [s2.enter_context(nc.sbuf_tensor(uq(f"wp{i}"), [128, 2, 512], BF16)) for i in range(2)]
            t_wp = [T(f"wp{i}") for i in range(2)]
            gs = [s2.enter_context(nc.sbuf_tensor(uq(f"gs{i}"), [128, 512], F32)) for i in range(2)]
            t_gs = [T(f"gs{i}") for i in range(2)]
            ob = [s2.enter_context(nc.sbuf_tensor(uq(f"ob{i}"), [128, D], F32)) for i in range(2)]
            t_ob = [T(f"ob{i}") for i in range(2)]
            load_gain(4)

            def final_out(t):
                so = t % 2
                norm_recip(t)
                k.op(k.dve, lambda: nc.vector.scalar_tensor_tensor(
                    out=ob[so][:], in0=H[:, t, :], scalar=rstd[:, t:t + 1], in1=gbc[:],
                    op0=ALU.mult, op1=ALU.mult), reads=[tH[t], t_rst[t], t_g], writes=[t_ob[so]])
                k.dma(k.sp, out[t * 128:(t + 1) * 128, :], ob[so][:], PT_("out"), reads=[t_ob[so]],
                      owner=t_ob[so])
            def load_g(cb):
                s = cb % 2
                src = w_pg[:, cb * 512:(cb + 1) * 512].rearrange("(c p) n -> p c n", p=128)
                if cb == 0:
                    for qq in range(4):
                        k.dma(k.pool, wg[s][:, qq * 4:(qq + 1) * 4, :], src[:, qq * 4:(qq + 1) * 4, :], t_wg[s][qq])
                else:
                    k.dma(k.pool, wg[s][:], src, t_wg[s][0], extra=t_wg[s][1:])
                k.dma(k.pool, wp[s][:], w_pp[:, cb * 512:(cb + 1) * 512].rearrange("(c p) n -> p c n", p=128),
                      t_wp[s])

            load_g(0)
            cnt = 0
            for cb in range(4):
                if cb + 1 < 4:
                    load_g(cb + 1)
                s = cb % 2
                for t in range(NT):
                    bg = (cnt % 2) * 2
                    bp = bg + 1
                    gsl = cnt % 2
                    cnt += 1
                    for c in range(NC_):
                        k.op(k.pe, lambda c=c, t=t, s=s, bg=bg: nc.tensor.matmul(
                            pf[bg][:, :], lhsT=xnT[:, c, t * 128:(t + 1) * 128], rhs=wg[s][:, c, :],
                            start=(c == 0), stop=(c == NC_ - 1)), reads=[t_wg[s][c // 4], tX[t]], writes=[tpf[bg]],
                            inc=(c == NC_ - 1))
                    for c in range(2):
                        k.op(k.pe, lambda c=c, t=t, s=s, bp=bp: nc.tensor.matmul(
                            pf[bp][:, :], lhsT=pT[:, c, t * 128:(t + 1) * 128], rhs=wp[s][:, c, :],
                            start=(c == 0), stop=(c == 1)), reads=[t_wp[s], t_pT], writes=[tpf[bp]], inc=(c == 1))
                    k.op(k.act, lambda bg=bg, gsl=gsl: nc.scalar.activation(out=gs[gsl][:], in_=pf[bg][:, :],
                                                                            func=AF.Sigmoid),
                         reads=[tpf[bg]], writes=[t_gs[gsl]])
                    k.op(k.dve, lambda bp=bp, gsl=gsl: nc.vector.tensor_tensor(out=gs[gsl][:], in0=pf[bp][:, :],
                                                                               in1=gs[gsl][:], op=ALU.mult),
                         reads=[tpf[bp], t_gs[gsl]], writes=[t_gs[gsl]])
                    k.op(k.dve, lambda t=t, cb=cb, gsl=gsl: nc.vector.tensor_tensor(
                        out=H[:, t, cb * 512:(cb + 1) * 512], in0=H[:, t, cb * 512:(cb + 1) * 512], in1=gs[gsl][:],
                        op=ALU.add), reads=[tH[t], t_gs[gsl]], writes=[tH[t]])
                    if cb == 3:
                        norm_stats_tile(t, xn[t % 2][:], t_xn[t % 2])
                        if t >= 1:
                            final_out(t - 1)
            final_out(NT - 1)
            k.barrier()

        print("instr counts", k.ninstr, "sems", k.nsem)
    return nc


_NC_CACHE = {}


def _host_constants():
    i = np.arange(128)
    cm = np.zeros((128, 3, 128), np.float32)
    cm[:, 0, :] = 1.0
    cm[:, 1, :] = (i[:, None] <= i[None, :]).astype(np.float32)
    cm[64, 2, :] = 1.0
    cmask = np.where(i[None, :] >= i[:, None], 0.0, NEG).astype(np.float32)
    kk = np.arange(128)[:, None] // 64
    qq = np.arange(512)[None, :] // 64
    am = np.zeros((8, 128, 512), np.float32)
    for di in range(8):
        d = 512 - 128 * di
        rel = d // 64 + qq - kk
        am[di] = np.where((rel >= 0) & (rel <= 8), 0.0, NEG / SCALE)
    sel8 = np.zeros((128, 8, 128), np.float32)
    for h in range(8):
        sel8[h, h, :] = 1.0
    return cm, cmask, am, sel8


def make_in_maps(x, p, g_ffn1, w_ffn1_gu, w_ffn1_down, g_mix, w_in, b_forget, rel_bias, w_out,
                 g_ffn2, w_ffn2_gu, w_ffn2_down, g_ple, w_ple_gate, w_ple_proj, g_final):
    f32 = np.float32
    x = np.asarray(x, f32)
    p = np.asarray(p, f32)
    cm, cmask, am, sel8c = _host_constants()
    gains = np.stack([np.broadcast_to(np.asarray(g, f32).reshape(1, D), (128, D))
                      for g in (g_ffn1[0], g_mix[0], g_ffn2[0], g_ple[0], g_final)]).astype(f32)
    bfg = np.ascontiguousarray(np.broadcast_to(np.asarray(b_forget, f32).reshape(1, 8), (128, 8)))
    kk = np.arange(128)[:, None]
    qq = np.arange(512)[None, :]
    idx = np.stack([np.clip(512 - 128 * di + qq - kk, -256, 256) + 256 for di in range(8)])
    rb_toep = np.ascontiguousarray(np.asarray(rel_bias, f32)[0][:, idx])
    shared = {
        "w1gu": np.ascontiguousarray(np.asarray(w_ffn1_gu, f32)[0]),
        "w1d": np.ascontiguousarray(np.asarray(w_ffn1_down, f32)[0]),
        "w2gu": np.ascontiguousarray(np.asarray(w_ffn2_gu, f32)[0]),
        "w2d": np.ascontiguousarray(np.asarray(w_ffn2_down, f32)[0]),
        "w_in": np.ascontiguousarray(np.asarray(w_in, f32)[0]),
        "w_out": np.ascontiguousarray(np.asarray(w_out, f32)[0]),
        "w_pg": np.ascontiguousarray(np.asarray(w_ple_gate, f32)[0]),
        "w_pp": np.ascontiguousarray(np.asarray(w_ple_proj, f32)[0]),
        "gains": gains, "bfg": bfg, "rb_toep": rb_toep, "amask": am, "cmask": cmask, "cmat": cm, "sel8c": sel8c,
    }
    in_maps = []
    zeros_prev = np.zeros((NTOK, D), f32)
    for c in range(8):
        b, half = c // 2, c % 2
        m = dict(shared)
        m["x_own"] = np.ascontiguousarray(x[b, half * NTOK:(half + 1) * NTOK])
        m["x_prev"] = np.ascontiguousarray(x[b, 0:NTOK]) if half == 1 else zeros_prev
        m["p_own"] = np.ascontiguousarray(p[0, b, half * NTOK:(half + 1) * NTOK])
        m["mprev"] = np.full((128, 1), 0.0 if half == 1 else NEG, f32)
        in_maps.append(m)
    return in_maps


def kernel(**inputs):
    f32 = np.float32
    if "nc" not in _NC_CACHE:
        _NC_CACHE["nc"] = build()
    nc = _NC_CACHE["nc"]
    in_maps = make_in_maps(**inputs)
    res = run_bass_kernel_spmd(nc, in_maps, core_ids=list(range(8)))
    outp = np.empty((4, 2 * NTOK, D), f32)
    for c in range(8):
        b, half = c // 2, c % 2
        outp[b, half * NTOK:(half + 1) * NTOK] = res.results[c]["out"]
    return outp
```

```python
import numpy as np
from contextlib import ExitStack
import concourse.bass as bass
import concourse.mybir as mybir
from concourse.bass_utils import run_bass_kernel_spmd

F32 = mybir.dt.float32
BF16 = mybir.dt.bfloat16
AF = mybir.ActivationFunctionType
ALU = mybir.AluOpType

D = 2048
DFF = 5632
NT = 8
NTOK = NT * 128
NC_ = D // 128
NJ = DFF // 128
DIN = 6152
DPLE = 256
EPS = 1e-6
NEG = -30000.0
SCALE = 128 ** -0.5
SEM_ROLL = 24000


class T:
    __slots__ = ("name", "w", "r", "dsem", "dcount", "scoped", "epoch")

    def __init__(self, name, scoped=True):
        self.name = name
        self.w = None
        self.r = []
        self.dsem = None
        self.dcount = 0
        self.scoped = scoped
        self.epoch = 0


def PT_(name):
    return T(name, scoped=False)


class Eng:
    def __init__(self, name, h):
        self.name = name
        self.h = h
        self.sem = None
        self.count = 0
        self.waited = {}
        self.pending = False
        self.mysems = set()
        self.last = None


class K:
    def __init__(self, nc, stack):
        self.nc = nc
        self.stack = stack
        self.nsem = 0
        self.pe = Eng("pe", nc.tensor)
        self.act = Eng("act", nc.scalar)
        self.dve = Eng("dve", nc.vector)
        self.pool = Eng("pool", nc.gpsimd)
        self.sp = Eng("sp", nc.sync)
        self.engs = [self.pe, self.act, self.dve, self.pool, self.sp]
        self.owners = []
        self.ninstr = {e.name: 0 for e in self.engs}
        self.epoch = 0
        self.scope_tok = {}
        self.scope_S = []

    def new_sem(self, name):
        self.nsem += 1
        return self.stack.enter_context(self.nc.semaphore(f"{name}_{self.nsem}"))

    def _wait(self, eng, tok):
        sem, val, tile = tok
        if eng.waited.get(sem, 0) >= val:
            return
        if eng.name == "pe" and sem in eng.mysems:
            return
        if tile is not None:
            assert tile.dcount == val, f"unsafe DMA wait on {tile.name}: {val} vs {tile.dcount}"
        eng.h.wait_ge(sem, val)
        self.ninstr[eng.name] += 1
        eng.waited[sem] = val

    def _wait_all(self, eng, deps):
        best = {}
        for d in deps:
            cur = best.get(d[0])
            if cur is None or d[1] > cur[1]:
                best[d[0]] = d
        for d in best.values():
            self._wait(eng, d)

    def _scope_deps(self, ts, deps):
        touched = False
        for t in ts:
            if t.scoped:
                touched = True
                if t.epoch < self.epoch:
                    deps.extend(self.scope_S)
                    t.epoch = self.epoch
        return touched

    def _scope_note(self, tok):
        cur = self.scope_tok.get(tok[0])
        if cur is None or tok[1] > cur[1]:
            self.scope_tok[tok[0]] = tok

    def op(self, eng, fn, reads=(), writes=(), inc=True):
        deps = []
        for t in reads:
            if t.w is not None:
                deps.append(t.w)
        for t in writes:
            if t.w is not None:
                deps.append(t.w)
            deps.extend(t.r)
        touched = self._scope_deps(tuple(reads) + tuple(writes), deps)
        self._wait_all(eng, deps)
        ins = fn()
        self.ninstr[eng.name] += 1
        if eng.sem is None or (eng.count >= SEM_ROLL and not eng.pending):
            eng.sem = self.new_sem(eng.name)
            eng.mysems.add(eng.sem)
            eng.count = 0
        if inc:
            eng.count += 1
            ins.then_inc(eng.sem, 1)
            tok = (eng.sem, eng.count, None)
            eng.pending = False
        else:
            tok = (eng.sem, eng.count + 1, None)
            eng.pending = True
        eng.last = tok
        if touched:
            self._scope_note(tok)
        for t in writes:
            t.w = tok
            t.r = []
        for t in reads:
            self._addr(t, tok)
        return ins

    @staticmethod
    def _addr(t, tok):
        for i, r in enumerate(t.r):
            if r[0] is tok[0]:
                if tok[1] > r[1]:
                    t.r[i] = tok
                return
        t.r.append(tok)

    def dma(self, eng, out_ap, in_ap, dst, reads=(), owner=None, extra=(), **kw):
        deps = []
        for t in reads:
            if t.w is not None:
                deps.append(t.w)
        ow = dst if owner is None else owner
        for d_ in (dst,) + tuple(extra):
            if d_.w is not None and not (d_.w[2] is ow and not d_.r):
                deps.append(d_.w)
            deps.extend(d_.r)
        self._scope_deps(tuple(reads) + (dst,) + tuple(extra), deps)
        self._wait_all(eng, deps)
        if ow.dsem is None:
            ow.dsem = self.new_sem("d" + ow.name)
            self.owners.append(ow)
        ins = eng.h.dma_start(out=out_ap, in_=in_ap, **kw)
        self.ninstr[eng.name] += 1
        ow.dcount += 16
        ins.then_inc(ow.dsem, 16)
        tok = (ow.dsem, ow.dcount, ow)
        for d_ in (dst,) + tuple(extra):
            d_.w = tok
            d_.r = []
        for t in reads:
            self._addr(t, tok)
        return ins

    def wait_tokens(self, eng, toks):
        self._wait_all(eng, list(toks))

    def scope_switch(self):
        assert not self.pe.pending
        for e in self.engs:
            for ow in self.owners:
                if ow.dcount > 0 and ow.scoped:
                    self._wait(e, (ow.dsem, ow.dcount, ow))
        self.scope_S = list(self.scope_tok.values())
        self.epoch += 1

    def barrier(self):
        assert not self.pe.pending
        for e in self.engs:
            for o in self.engs:
                if o is not e and o.last is not None and o.name not in ("sp",) and o.sem is not None and o.count > 0:
                    self._wait(e, (o.sem, o.count, None))
            for ow in self.owners:
                if ow.dcount > 0:
                    self._wait(e, (ow.dsem, ow.dcount, ow))


def build(stage=99):
    nc = bass.Bass("TRN2", target_bir_lowering=False)

    def din(name, shape, dt=F32):
        return nc.dram_tensor(name, list(shape), dt, kind="ExternalInput")

    x_own = din("x_own", [NTOK, D]).ap()
    x_prev = din("x_prev", [NTOK, D]).ap()
    p_own = din("p_own", [NTOK, DPLE]).ap()
    w1gu = din("w1gu", [D, 2 * DFF]).ap()
    w1d = din("w1d", [DFF, D]).ap()
    w2gu = din("w2gu", [D, 2 * DFF]).ap()
    w2d = din("w2d", [DFF, D]).ap()
    w_in = din("w_in", [D, DIN]).ap()
    w_out = din("w_out", [D, D]).ap()
    w_pg = din("w_pg", [D, D]).ap()
    w_pp = din("w_pp", [DPLE, D]).ap()
    gains = din("gains", [5, 128, D]).ap()
    bfg = din("bfg", [128, 8]).ap()
    rb_toep = din("rb_toep", [8, 8, 128, 512]).ap()
    amask_d = din("amask", [8, 128, 512]).ap()
    cmask_d = din("cmask", [128, 128]).ap()
    cmat_d = din("cmat", [128, 3, 128]).ap()
    mprev_d = din("mprev", [128, 1]).ap()
    SEL8_D = din("sel8c", [128, 8, 128]).ap()
    out = nc.dram_tensor("out", [NTOK, D], F32, kind="ExternalOutput").ap()

    s_qT = nc.dram_tensor("s_qT", [16, 128, NTOK], BF16, kind="Internal").ap()
    s_kT = nc.dram_tensor("s_kT", [2, 16, 128, NTOK], BF16, kind="Internal").ap()
    s_v = nc.dram_tensor("s_v", [2, NT, 128, D], BF16, kind="Internal").ap()

    _uid = [0]

    def uq(name):
        _uid[0] += 1
        return f"{name}_u{_uid[0]}"

    with ExitStack() as st:
        k = K(nc, st)

        def sb(name, shape, dt):
            return st.enter_context(nc.sbuf_tensor(uq(name), list(shape), dt))

        H = sb("H", [128, NT, D], F32)
        tH = [PT_(f"H{t}") for t in range(NT)]
        xnT = sb("xnT", [128, NC_, NTOK], BF16)
        tX = [PT_(f"xnT{t}") for t in range(NT)]
        ident = sb("ident", [128, 128], BF16)
        identf = sb("identf", [128, 128], F32)
        t_id = PT_("ident")
        cmat = sb("cmat", [128, 3, 128], F32)
        t_cmat = PT_("cmat")
        cmask = sb("cmask", [128, 128], F32)
        t_cmask = PT_("cmask")
        mprev = sb("mprev", [128, 1], F32)
        t_mprev = PT_("mprev")
        bfgs = sb("bfgs", [128, 8], F32)
        t_bfg = PT_("bfg")
        ones_bf = sb("ones_bf", [128, 128], BF16)
        t_ones = PT_("ones")
        ss = sb("ss", [128, NT], F32)
        rstd = sb("rstd", [128, NT], F32)
        t_ss = PT_("ss")
        t_rstd = PT_("rstd")
        Lp = sb("Lp", [128, NT, 8], F32)
        Lo = sb("Lo", [128, NT, 8], F32)
        t_Lp, t_Lo = PT_("Lp"), PT_("Lo")
        NFp = sb("NFp", [128, NT, 8], F32)
        NFo = sb("NFo", [128, NT, 8], F32)
        t_NFp, t_NFo = PT_("NFp"), PT_("NFo")
        b_prev = sb("b_prev", [128, 8, 8], F32)
        t_bprev = PT_("bprev")

        pf = [st.enter_context(nc.psum_tensor(f"pf{i}", [128, 512], F32)) for i in range(8)]
        tpf = [PT_(f"pf{i}") for i in range(8)]
        pb = [pf[6 + i][:].bitcast(BF16).rearrange("p (c n) -> p c n", n=128) for i in range(2)]
        tpb = [tpf[6], tpf[7]]

        k.op(k.pool, lambda: nc.gpsimd.memset(identf[:], 1.0), writes=[t_id])
        k.op(k.pool, lambda: nc.gpsimd.affine_select(out=identf[:], in_=identf[:], pattern=[[-1, 128]],
                                                     compare_op=ALU.is_equal, fill=0.0, base=0,
                                                     channel_multiplier=1), reads=[t_id], writes=[t_id])
        k.op(k.dve, lambda: nc.vector.tensor_copy(out=ident[:], in_=identf[:]), reads=[t_id], writes=[t_id])
        k.op(k.dve, lambda: nc.vector.memset(ones_bf[:], 1.0), writes=[t_ones])
        k.dma(k.sp, cmat[:], cmat_d, t_cmat)
        k.dma(k.sp, cmask[:], cmask_d, t_cmask)
        k.dma(k.sp, mprev[:], mprev_d, t_mprev)
        k.dma(k.sp, bfgs[:], bfg, t_bfg)

        alt = [0]

        def evac_engine():
            alt[0] ^= 1
            return k.act if alt[0] else k.dve

        def copy_on(eng, out_ap, in_ap, reads, writes):
            if eng is k.act:
                return k.op(k.act, lambda: nc.scalar.copy(out=out_ap, in_=in_ap), reads=reads, writes=writes)
            return k.op(k.dve, lambda: nc.vector.tensor_copy(out=out_ap, in_=in_ap), reads=reads, writes=writes)

        EPS_AP = sb("eps_ap", [128, 1], F32)
        t_eps = PT_("eps")
        k.op(k.dve, lambda: nc.vector.memset(EPS_AP[:], EPS), writes=[t_eps])
        xn = [sb(f"xn{i}", [128, D], BF16) for i in range(2)]
        t_xn = [PT_(f"xn{i}") for i in range(2)]
        gbc = sb("gbc", [128, D], F32)
        t_g = PT_("gbc")
        t_sst = [PT_(f"ss{t}") for t in range(NT)]
        t_rst = [PT_(f"rstd{t}") for t in range(NT)]

        def load_gain(gi):
            k.dma(k.sp, gbc[:], gains[gi], t_g)

        def norm_stats_tile(t, junk_ap, junk_t):
            k.op(k.act, lambda: nc.scalar.activation(out=junk_ap, in_=H[:, t, :], func=AF.Square,
                                                     accum_out=ss[:, t:t + 1]),
                 reads=[tH[t]], writes=[junk_t, t_sst[t]])
            k.op(k.act, lambda: nc.scalar.activation(out=rstd[:, t:t + 1], in_=ss[:, t:t + 1], func=AF.Sqrt,
                                                     scale=1.0 / D, bias=EPS_AP[:, 0:1]),
                 reads=[t_sst[t], t_eps], writes=[t_rst[t]])

        def norm_recip(t):
            k.op(k.dve, lambda: nc.vector.reciprocal(out=rstd[:, t:t + 1], in_=rstd[:, t:t + 1]),
                 reads=[t_rst[t]], writes=[t_rst[t]])

        def norm_part1a(t):
            s = t % 2
            norm_stats_tile(t, xn[s][:], t_xn[s])

        def norm_part1b(t):
            s = t % 2
            norm_recip(t)
            k.op(k.dve, lambda: nc.vector.scalar_tensor_tensor(
                out=xn[s][:], in0=H[:, t, :], scalar=rstd[:, t:t + 1], in1=gbc[:],
                op0=ALU.mult, op1=ALU.mult), reads=[tH[t], t_rst[t], t_g], writes=[t_xn[s]])

        def norm_step(t):
            if t - 2 >= 0 and t - 2 < NT:
                norm_part2(t - 2)
            if t < NT:
                norm_part1a(t)
            if t - 1 >= 0 and t - 1 < NT:
                norm_part1b(t - 1)

        def norm_part2(t):
            s = t % 2
            for half in range(2):
                for c in range(8):
                    cc = half * 8 + c
                    k.op(k.pe, lambda c=c, cc=cc, half=half: nc.tensor.transpose(
                        out=pb[half][:, c, :], in_=xn[s][:, cc * 128:(cc + 1) * 128], identity=ident[:]),
                        reads=[t_xn[s], t_id], writes=[tpb[half]], inc=(c == 7))
                k.op(k.act, lambda half=half: nc.scalar.copy(
                    out=xnT[:, half * 8:(half + 1) * 8, t * 128:(t + 1) * 128], in_=pb[half]),
                    reads=[tpb[half]], writes=[tX[t]])

        def norm_all():
            for t in range(NT + 2):
                norm_step(t)

        tile_free = [None]

        def rowproj_gen(wd_slot, t_wd, act_aps, act_tiles_fn, nrow, scale, pbase, with_norm=False):
            for t in range(NT):
                for cg in range(4):
                    bank = pbase + ((t * 4 + cg) % 2)
                    for jl in range(nrow):
                        k.op(k.pe, lambda t=t, cg=cg, jl=jl, bank=bank: nc.tensor.matmul(
                            pf[bank][:, :], lhsT=act_aps[jl][:, t * 128:(t + 1) * 128],
                            rhs=wd_slot[:, jl, cg * 512:(cg + 1) * 512],
                            start=(jl == 0), stop=(jl == nrow - 1)),
                            reads=[t_wd[cg] if isinstance(t_wd, list) else t_wd] + act_tiles_fn(t), writes=[tpf[bank]],
                            inc=(jl == nrow - 1))
                    k.op(k.dve, lambda t=t, cg=cg, bank=bank: nc.vector.scalar_tensor_tensor(
                        out=H[:, t, cg * 512:(cg + 1) * 512], in0=pf[bank][:, :], scalar=scale,
                        in1=H[:, t, cg * 512:(cg + 1) * 512], op0=ALU.mult, op1=ALU.add),
                        reads=[tpf[bank], tH[t]], writes=[tH[t]])
                    if not (with_norm and cg == 3):
                        yield
                if with_norm:
                    norm_step(t)
                    if tile_free[0] is not None and t >= 1:
                        tile_free[0](t - 1)
                    yield
            if with_norm:
                norm_step(NT)
                if tile_free[0] is not None:
                    tile_free[0](NT - 1)
                norm_step(NT + 1)

        def rowproj_pass(*a, **kw):
            for _ in rowproj_gen(*a, **kw):
                pass

        T_WGU = [[[T(f"wgu{i}a0"), T(f"wgu{i}a1")], [T(f"wgu{i}b0"), T(f"wgu{i}b1")]] for i in range(2)]
        T_WD = [T(f"wd{i}") for i in range(2)]
        T_WIN = [[T(f"win{i}_{j}") for j in range(4)] for i in range(2)]

        def ffn(wgu_d, wd_d, next_gain, entry_gain=None):
            with ExitStack() as s2:
                wgu = [s2.enter_context(nc.sbuf_tensor(uq(f"wgu{i}"), [128, NC_, 512], BF16)) for i in range(2)]
                t_wgu = T_WGU
                wd = [s2.enter_context(nc.sbuf_tensor(uq(f"wd{i}"), [128, 4, D], BF16)) for i in range(2)]
                t_wd = T_WD
                aT = [s2.enter_context(nc.sbuf_tensor(uq(f"aT{i}"), [128, 4, NTOK], BF16)) for i in range(2)]
                t_aT = [T(f"aT{i}") for i in range(2)]
                sg = [s2.enter_context(nc.sbuf_tensor(uq(f"sg{i}"), [128, 512], F32)) for i in range(2)]
                t_sg = [T(f"sg{i}") for i in range(2)]

                gu_done = set()
                d_done = set()

                def load_gu(idx):
                    if idx in gu_done:
                        return
                    gu_done.add(idx)
                    s = idx % 2
                    j0 = idx * 2
                    for ab in range(2):
                        cbase = ab * DFF + j0 * 128
                        if idx == 0:
                            for jj in range(2):
                                k.dma(k.pool, wgu[s][:, :, ab * 256 + jj * 128:ab * 256 + (jj + 1) * 128],
                                      wgu_d[:, cbase + jj * 128:cbase + (jj + 1) * 128].rearrange(
                                          "(c p) n -> p c n", p=128), t_wgu[s][ab][jj])
                        else:
                            k.dma(k.pool, wgu[s][:, :, ab * 256:(ab + 1) * 256],
                                  wgu_d[:, cbase:cbase + 256].rearrange("(c p) n -> p c n", p=128),
                                  t_wgu[s][ab][0], extra=[t_wgu[s][ab][1]])

                def load_d(pi):
                    if pi in d_done:
                        return
                    d_done.add(pi)
                    s = pi % 2
                    k.dma(k.pool, wd[s][:], wd_d[pi * 512:(pi + 1) * 512, :].rearrange("(j p) n -> p j n", p=128),
                          t_wd[s])

                NP = NJ // 4
                load_gu(0)
                if entry_gain is not None:
                    k.wait_tokens(k.pool, [tH[t].w for t in range(4) if tH[t].w is not None and tH[t].w[2] is not None])
                    load_gu(1)
                    load_d(0)
                    if entry_gain >= 0:
                        load_gain(entry_gain)
                st = {"cnt": 0, "prev": None}

                def up_step(pi, jp, jj, tg):
                    asl = pi % 2
                    s = (pi * 2 + jp) % 2
                    jl = jp * 2 + jj
                    bg = (st["cnt"] % 2) * 2
                    bu = bg + 1
                    st["cnt"] += 1
                    xt = tX[tg * 4:(tg + 1) * 4]
                    for c in range(NC_):
                        k.op(k.pe, lambda c=c: nc.tensor.matmul(
                            pf[bg][:, :], lhsT=wgu[s][:, c, jj * 128:(jj + 1) * 128],
                            rhs=xnT[:, c, tg * 512:(tg + 1) * 512], start=(c == 0), stop=(c == NC_ - 1)),
                            reads=[t_wgu[s][0][jj]] + xt, writes=[tpf[bg]], inc=(c == NC_ - 1))
                    for c in range(NC_):
                        k.op(k.pe, lambda c=c: nc.tensor.matmul(
                            pf[bu][:, :], lhsT=wgu[s][:, c, 256 + jj * 128:256 + (jj + 1) * 128],
                            rhs=xnT[:, c, tg * 512:(tg + 1) * 512], start=(c == 0), stop=(c == NC_ - 1)),
                            reads=[t_wgu[s][1][jj]] + xt, writes=[tpf[bu]], inc=(c == NC_ - 1))
                    ss_ = st["cnt"] % 2
                    k.op(k.act, lambda: nc.scalar.activation(out=sg[ss_][:], in_=pf[bg][:, :], func=AF.Silu),
                         reads=[tpf[bg]], writes=[t_sg[ss_]])
                    k.op(k.dve, lambda: nc.vector.tensor_tensor(
                        out=aT[asl][:, jl, tg * 512:(tg + 1) * 512], in0=pf[bu][:, :], in1=sg[ss_][:],
                        op=ALU.mult), reads=[tpf[bu], t_sg[ss_]], writes=[t_aT[asl]])
                    if st["prev"] is not None:
                        for _ in range(4):
                            if next(st["prev"], "done") == "done":
                                st["prev"] = None
                                break

                for pi in range(NP):
                    asl = pi % 2
                    if pi == 0 and entry_gain is not None:
                        for t in range(6):
                            norm_step(t)
                        for jp in range(2):
                            for jj in range(2):
                                up_step(0, jp, jj, 0)
                        for t in range(6, NT + 2):
                            norm_step(t)
                        for jp in range(2):
                            for jj in range(2):
                                up_step(0, jp, jj, 1)
                            load_gu(jp + 2)
                    else:
                        for jp in range(2):
                            idx = pi * 2 + jp
                            if idx + 1 < NP * 2:
                                load_gu(idx + 1)
                            if jp == 0:
                                load_d(pi)
                            for jj in range(2):
                                for tg in range(2):
                                    up_step(pi, jp, jj, tg)
                    if pi == NP - 2:
                        load_gain(next_gain)
                    if st["prev"] is not None:
                        for _ in st["prev"]:
                            pass
                    st["prev"] = rowproj_gen(wd[pi % 2], t_wd[pi % 2], [aT[asl][:, jl, :] for jl in range(4)],
                                             lambda t, asl=asl: [t_aT[asl]], 4, 0.5, 4, with_norm=(pi == NP - 1))
                for _ in st["prev"]:
                    pass
                k.scope_switch()

        def inproj(which, after_first=None):
            with ExitStack() as s2:
                win = [s2.enter_context(nc.sbuf_tensor(uq(f"win{i}"), [128, NC_, 512], BF16)) for i in range(2)]
                t_win = T_WIN
                wf = s2.enter_context(nc.sbuf_tensor(uq("wf"), [128, NC_, 8], BF16))
                t_wf = T("wf")
                zst = [s2.enter_context(nc.sbuf_tensor(uq(f"zst{i}"), [128, NTOK], BF16)) for i in range(2)]
                t_zst = [T(f"zst{i}") for i in range(2)]
                vst = [s2.enter_context(nc.sbuf_tensor(uq(f"vst{i}"), [128, 512], BF16)) for i in range(4)]
                t_vst = [T(f"vst{i}") for i in range(4)]
                xf = s2.enter_context(nc.sbuf_tensor(uq("xf"), [128, 8], F32))
                t_xf = T("xf")
                L = Lo if which else Lp
                t_L = t_Lo if which else t_Lp
                t_scr = PT_("scr")
                blocks = []
                if which:
                    blocks += [("q", 0, 0), ("q", 1, 4)]
                blocks += [("k", 2, 0), ("k", 3, 4), ("v", 4, 0), ("v", 5, 4)]
                if which:
                    blocks += [("q", 6, 8), ("q", 7, 12)]
                blocks += [("k", 8, 8), ("k", 9, 12), ("v", 10, 8), ("v", 11, 12)]

                def load_blk(i):
                    kind, cb, h0 = blocks[i]
                    s = i % 2
                    if i == 0:
                        for hh in range(4):
                            k.dma(k.pool, win[s][:, :, hh * 128:(hh + 1) * 128],
                                  w_in[:, cb * 512 + hh * 128:cb * 512 + (hh + 1) * 128].rearrange(
                                      "(c p) n -> p c n", p=128), t_win[s][hh])
                    else:
                        k.dma(k.pool, win[s][:], w_in[:, cb * 512:(cb + 1) * 512].rearrange(
                            "(c p) n -> p c n", p=128), t_win[s][0], extra=t_win[s][1:])

                def emit_flogits():
                    for t in range(NT):
                        bank = 4 + (t % 2)
                        for c in range(NC_):
                            k.op(k.pe, lambda c=c, t=t, bank=bank: nc.tensor.matmul(
                                pf[bank][:, 0:8], lhsT=xnT[:, c, t * 128:(t + 1) * 128], rhs=wf[:, c, :],
                                start=(c == 0), stop=(c == NC_ - 1)),
                                reads=[t_wf, tX[t]], writes=[tpf[bank]], inc=(c == NC_ - 1))
                        k.op(k.dve, lambda t=t, bank=bank: nc.vector.tensor_tensor(
                            out=L[:, t, :], in0=pf[bank][:, 0:8], in1=bfgs[:], op=ALU.add),
                            reads=[tpf[bank], t_bfg], writes=[t_L])
                    k.op(k.act, lambda: nc.scalar.activation(out=L[:], in_=L[:], func=AF.Exp, scale=-1.0),
                         reads=[t_L], writes=[t_L])
                    k.op(k.act, lambda: nc.scalar.activation(out=L[:], in_=L[:], func=AF.Ln, bias=ONE_AP[:, 0:1]),
                         reads=[t_L, t_one], writes=[t_L])
                def emit_cumsum():
                    NF = NFo if which else NFp
                    t_NF = t_NFo if which else t_NFp
                    for t in range(NT):
                        bank = 4 + (t % 2)
                        for tp in range(t + 1):
                            mi = 1 if tp == t else 0
                            k.op(k.pe, lambda t=t, tp=tp, mi=mi, bank=bank: nc.tensor.matmul(
                                pf[bank][:, 0:8], lhsT=cmat[:, mi, :], rhs=L[:, tp, :], start=(tp == 0), stop=(tp == t)),
                                reads=[t_cmat, t_L], writes=[tpf[bank]], inc=(tp == t))
                        k.op(k.dve, lambda t=t, bank=bank: nc.vector.tensor_copy(out=NF[:, t, :], in_=pf[bank][:, 0:8]),
                             reads=[tpf[bank]], writes=[t_NF])

                load_blk(0)
                zc = 0
                vc = 0
                bc = 0
                for i, (kind, cb, h0) in enumerate(blocks):
                    if i + 1 < len(blocks):
                        load_blk(i + 1)
                    if i == 1:
                        with nc.allow_non_contiguous_dma(reason="tiny forget-gate weight columns"):
                            k.dma(k.pool, wf[:], w_in[:, 6144:6152].rearrange("(c p) n -> p c n", p=128), t_wf)
                        if after_first is not None:
                            after_first()
                    if i == 2:
                        emit_flogits()
                    if i == 4:
                        emit_cumsum()
                    s = i % 2
                    if kind in ("q", "k"):
                        for hh in range(4):
                            h = h0 + hh
                            zs = zc % 2
                            zc += 1
                            tgs = [1] if (which == 0 and h < 8) else [0, 1]
                            for tg in tgs:
                                bank = bc % 4
                                bc += 1
                                for c in range(NC_):
                                    k.op(k.pe, lambda c=c, s=s, hh=hh, tg=tg, bank=bank: nc.tensor.matmul(
                                        pf[bank][:, :], lhsT=win[s][:, c, hh * 128:(hh + 1) * 128],
                                        rhs=xnT[:, c, tg * 512:(tg + 1) * 512], start=(c == 0), stop=(c == NC_ - 1)),
                                        reads=[t_win[s][hh]] + tX[tg * 4:(tg + 1) * 4], writes=[tpf[bank]],
                                        inc=(c == NC_ - 1))
                                copy_on(evac_engine(), zst[zs][:, tg * 512:(tg + 1) * 512], pf[bank][:, :],
                                        [tpf[bank]], [t_zst[zs]])
                            dst = s_qT[h] if kind == "q" else s_kT[which, h]
                            if len(tgs) == 1:
                                k.dma(k.sp, dst[:, 512:1024], zst[zs][:, 512:1024], PT_("scr"), reads=[t_zst[zs]],
                                      owner=t_zst[zs])
                            else:
                                k.dma(k.sp, dst, zst[zs][:], PT_("scr"), reads=[t_zst[zs]], owner=t_zst[zs])
                    else:
                        for t in (range(4, NT) if (which == 0 and h0 < 8) else range(NT)):
                            bank = bc % 4
                            bc += 1
                            vs = vc % 4
                            vc += 1
                            for c in range(NC_):
                                k.op(k.pe, lambda c=c, s=s, t=t, bank=bank: nc.tensor.matmul(
                                    pf[bank][:, :], lhsT=xnT[:, c, t * 128:(t + 1) * 128], rhs=win[s][:, c, :],
                                    start=(c == 0), stop=(c == NC_ - 1)),
                                    reads=t_win[s] + [tX[t]], writes=[tpf[bank]], inc=(c == NC_ - 1))
                            copy_on(evac_engine(), vst[vs][:], pf[bank][:, :], [tpf[bank]], [t_vst[vs]])
                            k.dma(k.sp, s_v[which, t, :, h0 * 128:h0 * 128 + 512], vst[vs][:], PT_("scr"),
                                  reads=[t_vst[vs]], owner=t_vst[vs])
                k.scope_switch()

        ONE_AP = sb("one_ap", [128, 1], F32)
        t_one = PT_("one")
        k.op(k.dve, lambda: nc.vector.memset(ONE_AP[:], 1.0), writes=[t_one])

        def load_H(src):
            for t in range(NT):
                k.dma(k.sp, H[:, t, :], src[t * 128:(t + 1) * 128, :], tH[t])

        load_gain(0)
        load_H(x_prev)
        tile_free[0] = lambda t: k.dma(k.sp, H[:, t, :], x_own[t * 128:(t + 1) * 128, :], tH[t])
        ffn(w1gu, w1d, 1, entry_gain=-1)
        tile_free[0] = None
        inproj(0)

        ffn(w1gu, w1d, 1, entry_gain=0)
        inproj(1)

        pT = sb("pT", [128, 2, NTOK], BF16)
        t_pT = PT_("pT")
        with ExitStack() as fx:
            pall = fx.enter_context(nc.sbuf_tensor(uq("pall"), [128, NT, DPLE], BF16))
            t_pall = T("pall")
            k.dma(k.pool, pall[:], p_own.rearrange("(t p) d -> p t d", p=128), t_pall)
            for t0 in range(0, NT, 4):
                for c in range(2):
                    for tt in range(4):
                        k.op(k.pe, lambda c=c, tt=tt, t0=t0: nc.tensor.transpose(
                            out=pb[1][:, c * 4 + tt, :], in_=pall[:, t0 + tt, c * 128:(c + 1) * 128],
                            identity=ident[:]),
                            reads=[t_pall, t_id], writes=[tpb[1]], inc=(c == 1 and tt == 3))
                for c in range(2):
                    k.op(k.act, lambda c=c, t0=t0: nc.scalar.copy(
                        out=pT[:, c, t0 * 128:(t0 + 4) * 128],
                        in_=pb[1][:, c * 4:(c + 1) * 4, :]), reads=[tpb[1]], writes=[t_pT])
            sel8_d = SEL8_D
            sel8 = fx.enter_context(nc.sbuf_tensor(uq("sel8"), [128, 8, 128], BF16))
            t_sel8 = T("sel8")
            nfrow = fx.enter_context(nc.sbuf_tensor(uq("nfrow"), [128, NTOK], BF16))
            t_nfrow = T("nfrow")
            cmask_bf = fx.enter_context(nc.sbuf_tensor(uq("cmask_bf"), [128, 128], BF16))
            t_cmbf = T("cmask_bf")

            oT = xnT
            with ExitStack() as s2:
                def sbt(name, shape, dt):
                    return s2.enter_context(nc.sbuf_tensor(uq(name), list(shape), dt))
                qT = [sbt(f"qT{i}", [128, NTOK], BF16) for i in range(2)]
                kTo = [sbt(f"kTo{i}", [128, NTOK], BF16) for i in range(2)]
                kTp = [sbt(f"kTp{i}", [128, NTOK], BF16) for i in range(2)]
                vo = [sbt(f"vo{i}", [128, NT, 128], BF16) for i in range(2)]
                vp = [sbt(f"vp{i}", [128, NT, 128], BF16) for i in range(2)]
                t_hd = [[T(f"hd{i}_{j}") for j in range(5)] for i in range(2)]
                bmf = sbt("bmf", [128, 8, 512], F32)
                t_bmf = [T(f"bmf{j}") for j in range(8)]
                bmb = [sbt(f"bmb{i}", [128, 8, 512], BF16) for i in range(2)]
                t_bmb = [T(f"bmb{i}") for i in range(2)]
                amask = sbt("amask_s", [128, 8, 512], F32)
                t_am = T("amask")
                NPT = 4
                PT = [sbt(f"PT{i}", [128, 512], BF16) for i in range(NPT)]
                t_PT = [T(f"PT{i}") for i in range(NPT)]
                NST = 3
                rden = [sbt(f"rden{i}", [128, 512], F32) for i in range(2)]
                t_rden = [T(f"rden{i}") for i in range(2)]
                zero_ap = sbt("zero_ap", [128, 1], F32)
                t_zero = T("zero")
                k.op(k.dve, lambda: nc.vector.memset(zero_ap[:], 0.0), writes=[t_zero])
                k.dma(k.sp, amask[:], amask_d.rearrange("d p n -> p d n"), t_am)

                def load_head(h, s):
                    with nc.allow_non_contiguous_dma(reason="per-head v slices"):
                        if h < 8:
                            k.dma(k.sp, bmf[:], rb_toep[h].rearrange("d p n -> p d n"), t_bmf[0])
                        k.dma(k.sp, qT[s][:], s_qT[h], t_hd[s][0])
                        k.dma(k.sp, kTo[s][:], s_kT[1, h], t_hd[s][1])
                        if h < 8:
                            k.dma(k.sp, kTp[s][:, 512:1024], s_kT[0, h][:, 512:1024], t_hd[s][2])
                        else:
                            k.dma(k.sp, kTp[s][:], s_kT[0, h], t_hd[s][2])
                        k.dma(k.sp, vo[s][:], s_v[1, :, :, h * 128:(h + 1) * 128].rearrange("t p d -> p t d"),
                              t_hd[s][3])
                        if h < 8:
                            k.dma(k.sp, vp[s][:, 4:8, :],
                                  s_v[0, 4:8, :, h * 128:(h + 1) * 128].rearrange("t p d -> p t d"), t_hd[s][4])
                        else:
                            k.dma(k.sp, vp[s][:], s_v[0, :, :, h * 128:(h + 1) * 128].rearrange("t p d -> p t d"),
                                  t_hd[s][4])

                NSC = 4
                units = []
                for hi in range(16):
                    h = hi
                    s = hi % 2
                    for qg in range(2):
                        Q0 = qg * 512
                        grp = []
                        if h < 8:
                            for di in range(8):
                                K0 = Q0 - 512 + 128 * di
                                if K0 < 0:
                                    grp.append(dict(kind="A", kt=kTp[s], vt=vp[s], kb=(NTOK + K0) // 128, c0=0, di=di,
                                                    prev=True))
                                else:
                                    grp.append(dict(kind="A", kt=kTo[s], vt=vo[s], kb=K0 // 128, c0=0, di=di,
                                                    prev=False))
                        else:
                            for kb in range(8):
                                grp.append(dict(kind="B", kt=kTp[s], vt=vp[s], kb=kb, c0=0, di=None, prev=True))
                            for kb in range(4 * qg + 4):
                                grp.append(dict(kind="B", kt=kTo[s], vt=vo[s], kb=kb, c0=max(0, kb - 4 * qg), di=None,
                                                prev=False, diag=(kb >= 4 * qg)))
                        for gi_, u in enumerate(grp):
                            u.update(h=h, s=s, qg=qg, Q0=Q0, first=(gi_ == 0), last=(gi_ == len(grp) - 1),
                                     gidx=hi * 2 + qg, newhead=(qg == 0 and gi_ == 0), hi=hi)
                            units.append(u)
                for i_, u in enumerate(units):
                    u["sc"] = i_ % NSC
                    u["pt"] = i_ % NPT
                    u["st"] = i_ % NST

                def emit_qk(u):
                    s, c0, Q0, sc = u["s"], u["c0"], u["Q0"], u["sc"]
                    kb = u["kb"]
                    ncols = slice(c0 * 128, 512)
                    extra = []
                    if u["kind"] == "A":
                        extra.append("bias")
                    if u["kind"] == "B":
                        extra.append("aug")
                        if u.get("diag"):
                            extra.append("mask")
                    k.op(k.pe, lambda: nc.tensor.matmul(
                        pf[sc][:, ncols], lhsT=u["kt"][:, kb * 128:(kb + 1) * 128],
                        rhs=qT[s][:, Q0 + c0 * 128:Q0 + 512], start=True, stop=(not extra)),
                        reads=t_hd[s], writes=[tpf[sc]], inc=(not extra))
                    if "bias" in extra:
                        di = u["di"]
                        k.op(k.pe, lambda: nc.tensor.matmul(
                            pf[sc][:, :], lhsT=ident[:], rhs=bmb[s][:, di, :], start=False, stop=True),
                            reads=[t_id, t_bmb[s]], writes=[tpf[sc]], inc=True)
                    if "aug" in extra:
                        hb = u["h"] - 8
                        lastx = ("mask" not in extra)
                        k.op(k.pe, lambda: nc.tensor.matmul(
                            pf[sc][:, ncols], lhsT=sel8[:, hb, :], rhs=nfrow[:, Q0 + c0 * 128:Q0 + 512],
                            start=False, stop=lastx), reads=[t_sel8, t_nfrow], writes=[tpf[sc]], inc=lastx)
                    if "mask" in extra:
                        k.op(k.pe, lambda: nc.tensor.matmul(
                            pf[sc][:, c0 * 128:(c0 + 1) * 128], lhsT=ident[:], rhs=cmask_bf[:],
                            start=False, stop=True), reads=[t_id, t_cmbf], writes=[tpf[sc]], inc=True)

                def emit_soft(u):
                    s, c0, sc, pt = u["s"], u["c0"], u["sc"], u["pt"]
                    kb = u["kb"]
                    if u["kind"] == "A":
                        bias_ap = mprev[:, 0:1] if u["prev"] else zero_ap[:, 0:1]
                        k.op(k.act, lambda: nc.scalar.activation(out=PT[pt][:], in_=pf[sc][:, :], func=AF.Exp,
                                                                 scale=SCALE, bias=bias_ap),
                             reads=[tpf[sc], t_mprev, t_zero], writes=[t_PT[pt]])
                    else:
                        hb = u["h"] - 8
                        bias_ap = b_prev[:, kb, hb:hb + 1] if u["prev"] else NFo[:, kb, hb:hb + 1]
                        k.op(k.act, lambda: nc.scalar.activation(
                            out=PT[pt][:, c0 * 128:512], in_=pf[sc][:, c0 * 128:512], func=AF.Exp, scale=SCALE,
                            bias=bias_ap), reads=[tpf[sc], t_bprev, t_NFo], writes=[t_PT[pt]])

                def emit_pv(u):
                    s, c0, pt = u["s"], u["c0"], u["pt"]
                    po = 4 + (u["gidx"] % 2) * 2
                    pd_ = po + 1
                    k.op(k.pe, lambda: nc.tensor.matmul(
                        pf[po][:, c0 * 128:512], lhsT=u["vt"][:, u["kb"], :], rhs=PT[pt][:, c0 * 128:512],
                        start=u["first"], stop=u["last"]), reads=t_hd[s] + [t_PT[pt]], writes=[tpf[po]], inc=False)
                    k.op(k.pe, lambda: nc.tensor.matmul(
                        pf[pd_][:, c0 * 128:512], lhsT=ones_bf[:], rhs=PT[pt][:, c0 * 128:512],
                        start=u["first"], stop=u["last"]), reads=[t_ones, t_PT[pt]], writes=[tpf[pd_]], inc=True)
                    if u["last"]:
                        r_ = u["gidx"] % 2
                        h, Q0, qg = u["h"], u["Q0"], u["qg"]
                        k.op(k.dve, lambda: nc.vector.reciprocal(out=rden[r_][:], in_=pf[pd_][:, :]),
                             reads=[tpf[pd_]], writes=[t_rden[r_]])
                        k.op(k.dve, lambda: nc.vector.tensor_tensor(
                            out=oT[:, h, Q0:Q0 + 512], in0=pf[po][:, :], in1=rden[r_][:], op=ALU.mult),
                            reads=[tpf[po], t_rden[r_]], writes=tX[qg * 4:(qg + 1) * 4])

                LA = 3
                loaded = set()

                def ensure_head(hi):
                    if hi < 16 and hi not in loaded:
                        loaded.add(hi)
                        load_head(hi, hi % 2)
                        if hi < 8:
                            s_ = hi % 2
                            k.op(k.dve, lambda: nc.vector.scalar_tensor_tensor(
                                out=bmb[s_][:], in0=bmf[:], scalar=1.0 / SCALE, in1=amask[:], op0=ALU.mult, op1=ALU.add),
                                reads=[t_bmf[0], t_am], writes=[t_bmb[s_]])

                ensure_head(0)
                ntp = s2.enter_context(nc.sbuf_tensor(uq("ntp"), [128, 8], F32))
                nfb = s2.enter_context(nc.sbuf_tensor(uq("nfb"), [128, NT, 8], BF16))
                t_ntp, t_nfb = T("ntp"), T("nfb")
                k.dma(k.pool, sel8[:], sel8_d, t_sel8)
                k.op(k.dve, lambda: nc.vector.tensor_copy(out=cmask_bf[:], in_=cmask[:]), reads=[t_cmask],
                     writes=[t_cmbf])
                k.op(k.dve, lambda: nc.vector.memset(nfrow[:], 0.0), writes=[t_nfrow])
                k.op(k.dve, lambda: nc.vector.tensor_scalar(out=nfb[:], in0=NFo[:], scalar1=-1.0 / SCALE, scalar2=None,
                                                            op0=ALU.mult), reads=[t_NFo], writes=[t_nfb])
                for t in range(NT):
                    k.op(k.pe, lambda t=t: nc.tensor.transpose(out=pb[0][0:8, t, :], in_=nfb[:, t, :], identity=ident[:]),
                         reads=[t_nfb, t_id], writes=[tpb[0]], inc=(t == NT - 1))
                k.op(k.act, lambda: nc.scalar.copy(out=nfrow[0:8, :].rearrange("p (t n) -> p t n", n=128),
                                                   in_=pb[0][0:8, :, :]), reads=[tpb[0]], writes=[t_nfrow])
                for t in range(NT):
                    k.op(k.pe, lambda t=t: nc.tensor.matmul(pf[1][:, 0:8], lhsT=cmat[:, 0, :], rhs=Lp[:, t, :],
                                                            start=(t == 0), stop=(t == NT - 1)),
                         reads=[t_cmat, t_Lp], writes=[tpf[1]], inc=(t == NT - 1))
                k.op(k.dve, lambda: nc.vector.tensor_copy(out=ntp[:], in_=pf[1][:, 0:8]), reads=[tpf[1]], writes=[t_ntp])
                k.op(k.dve, lambda: nc.vector.tensor_scalar(out=ntp[:], in0=ntp[:], scalar1=mprev[:, 0:1], scalar2=None,
                                                            op0=ALU.subtract), reads=[t_ntp, t_mprev], writes=[t_ntp])
                for kb in range(8):
                    k.op(k.dve, lambda kb=kb: nc.vector.tensor_tensor(out=b_prev[:, kb, :], in0=NFp[:, kb, :], in1=ntp[:],
                                                                      op=ALU.subtract),
                         reads=[t_NFp, t_ntp], writes=[t_bprev])
                n = len(units)
                for i_ in range(min(LA, n)):
                    emit_qk(units[i_])
                for i_ in range(n):
                    u = units[i_]
                    if u["newhead"]:
                        ensure_head(u["hi"] + 1)
                    emit_soft(u)
                    if i_ + LA < n:
                        emit_qk(units[i_ + LA])
                    emit_pv(u)
                k.scope_switch()

        with ExitStack() as s2:
            wo = [s2.enter_context(nc.sbuf_tensor(uq(f"wo{i}"), [128, 4, D], BF16)) for i in range(2)]
            t_wo = [[T(f"wo{i}_{c}") for c in range(4)] for i in range(2)]

            def load_o(pi):
                src = w_out[pi * 512:(pi + 1) * 512, :].rearrange("(j p) n -> p j n", p=128)
                if pi == 0:
                    for cg in range(4):
                        k.dma(k.pool, wo[0][:, :, cg * 512:(cg + 1) * 512], src[:, :, cg * 512:(cg + 1) * 512],
                              t_wo[0][cg])
                else:
                    k.dma(k.pool, wo[pi % 2][:], src, t_wo[pi % 2][0], extra=t_wo[pi % 2][1:])

            load_o(0)
            load_gain(2)
            for pi in range(4):
                if pi + 1 < 4:
                    load_o(pi + 1)
                rowproj_pass(wo[pi % 2], t_wo[pi % 2], [oT[:, pi * 4 + jl, :] for jl in range(4)],
                             lambda t: [tX[t]], 4, 1.0, 4, with_norm=(pi == 3))
            k.scope_switch()

        ffn(w2gu, w2d, 3)

        with ExitStack() as s2:
            wg = [s2.enter_context(nc.sbuf_tensor(uq(f"wg{i}"), [128, NC_, 512], BF16)) for i in range(2)]
            t_wg = [[T(f"wg{i}_{q}") for q in range(4)] for i in range(2)]
            wp = [s2.enter_context(nc.sbuf_tensor(uq(f"wp{i}"), [128, 2, 512], BF16)) for i in range(2)]
            t_wp = [T(f"wp{i}") for i in range(2)]
            gs = [s2.enter_context(nc.sbuf_tensor(uq(f"gs{i}"), [128, 512], F32)) for i in range(2)]
            t_gs = [T(f"gs{i}") for i in range(2)]
            ob = [s2.enter_context(nc.sbuf_tensor(uq(f"ob{i}"), [128, D], F32)) for i in range(2)]
            t_ob = [T(f"ob{i}") for i in range(2)]
            load_gain(4)

            def final_out(t):
                so = t % 2
                norm_recip(t)
                k.op(k.dve, lambda: nc.vector.scalar_tensor_tensor(
                    out=ob[so][:], in0=H[:, t, :], scalar=rstd[:, t:t + 1], in1=gbc[:],
                    op0=ALU.mult, op1=ALU.mult), reads=[tH[t], t_rst[t], t_g], writes=[t_ob[so]])
                k.dma(k.sp, out[t * 128:(t + 1) * 128, :], ob[so][:], PT_("out"), reads=[t_ob[so]],
                      owner=t_ob[so])
            def load_g(cb):
                s = cb % 2
                src = w_pg[:, cb * 512:(cb + 1) * 512].rearrange("(c p) n -> p c n", p=128)
                if cb == 0:
                    for qq in range(4):
                        k.dma(k.pool, wg[s][:, qq * 4:(qq + 1) * 4, :], src[:, qq * 4:(qq + 1) * 4, :], t_wg[s][qq])
                else:
                    k.dma(k.pool, wg[s][:], src, t_wg[s][0], extra=t_wg[s][1:])
                k.dma(k.pool, wp[s][:], w_pp[:, cb * 512:(cb + 1) * 512].rearrange("(c p) n -> p c n", p=128),
                      t_wp[s])

            load_g(0)
            cnt = 0
            for cb in range(4):
                if cb + 1 < 4:
                    load_g(cb + 1)
                s = cb % 2
                for t in range(NT):
                    bg = (cnt % 2) * 2
                    bp = bg + 1
                    gsl = cnt % 2
                    cnt += 1
                    for c in range(NC_):
                        k.op(k.pe, lambda c=c, t=t, s=s, bg=bg: nc.tensor.matmul(
                            pf[bg][:, :], lhsT=xnT[:, c, t * 128:(t + 1) * 128], rhs=wg[s][:, c, :],
                            start=(c == 0), stop=(c == NC_ - 1)), reads=[t_wg[s][c // 4], tX[t]], writes=[tpf[bg]],
                            inc=(c == NC_ - 1))
                    for c in range(2):
                        k.op(k.pe, lambda c=c, t=t, s=s, bp=bp: nc.tensor.matmul(
                            pf[bp][:, :], lhsT=pT[:, c, t * 128:(t + 1) * 128], rhs=wp[s][:, c, :],
                            start=(c == 0), stop=(c == 1)), reads=[t_wp[s], t_pT], writes=[tpf[bp]], inc=(c == 1))
                    k.op(k.act, lambda bg=bg, gsl=gsl: nc.scalar.activation(out=gs[gsl][:], in_=pf[bg][:, :],
                                                                            func=AF.Sigmoid),
                         reads=[tpf[bg]], writes=[t_gs[gsl]])
                    k.op(k.dve, lambda bp=bp, gsl=gsl: nc.vector.tensor_tensor(out=gs[gsl][:], in0=pf[bp][:, :],
                                                                               in1=gs[gsl][:], op=ALU.mult),
                         reads=[tpf[bp], t_gs[gsl]], writes=[t_gs[gsl]])
                    k.op(k.dve, lambda t=t, cb=cb, gsl=gsl: nc.vector.tensor_tensor(
                        out=H[:, t, cb * 512:(cb + 1) * 512], in0=H[:, t, cb * 512:(cb + 1) * 512], in1=gs[gsl][:],
                        op=ALU.add), reads=[tH[t], t_gs[gsl]], writes=[tH[t]])
                    if cb == 3:
                        norm_stats_tile(t, xn[t % 2][:], t_xn[t % 2])
                        if t >= 1:
                            final_out(t - 1)
            final_out(NT - 1)
            k.barrier()

        print("instr counts", k.ninstr, "sems", k.nsem)
    return nc


_NC_CACHE = {}


def _host_constants():
    i = np.arange(128)
    cm = np.zeros((128, 3, 128), np.float32)
    cm[:, 0, :] = 1.0
    cm[:, 1, :] = (i[:, None] <= i[None, :]).astype(np.float32)
    cm[64, 2, :] = 1.0
    cmask = np.where(i[None, :] >= i[:, None], 0.0, NEG).astype(np.float32)
    kk = np.arange(128)[:, None] // 64
    qq = np.arange(512)[None, :] // 64
    am = np.zeros((8, 128, 512), np.float32)
    for di in range(8):
        d = 512 - 128 * di
        rel = d // 64 + qq - kk
        am[di] = np.where((rel >= 0) & (rel <= 8), 0.0, NEG / SCALE)
    sel8 = np.zeros((128, 8, 128), np.float32)
    for h in range(8):
        sel8[h, h, :] = 1.0
    return cm, cmask, am, sel8


def make_in_maps(x, p, g_ffn1, w_ffn1_gu, w_ffn1_down, g_mix, w_in, b_forget, rel_bias, w_out,
                 g_ffn2, w_ffn2_gu, w_ffn2_down, g_ple, w_ple_gate, w_ple_proj, g_final):
    f32 = np.float32
    x = np.asarray(x, f32)
    p = np.asarray(p, f32)
    cm, cmask, am, sel8c = _host_constants()
    gains = np.stack([np.broadcast_to(np.asarray(g, f32).reshape(1, D), (128, D))
                      for g in (g_ffn1[0], g_mix[0], g_ffn2[0], g_ple[0], g_final)]).astype(f32)
    bfg = np.ascontiguousarray(np.broadcast_to(np.asarray(b_forget, f32).reshape(1, 8), (128, 8)))
    kk = np.arange(128)[:, None]
    qq = np.arange(512)[None, :]
    idx = np.stack([np.clip(512 - 128 * di + qq - kk, -256, 256) + 256 for di in range(8)])
    rb_toep = np.ascontiguousarray(np.asarray(rel_bias, f32)[0][:, idx])
    shared = {
        "w1gu": np.ascontiguousarray(np.asarray(w_ffn1_gu, f32)[0]),
        "w1d": np.ascontiguousarray(np.asarray(w_ffn1_down, f32)[0]),
        "w2gu": np.ascontiguousarray(np.asarray(w_ffn2_gu, f32)[0]),
        "w2d": np.ascontiguousarray(np.asarray(w_ffn2_down, f32)[0]),
        "w_in": np.ascontiguousarray(np.asarray(w_in, f32)[0]),
        "w_out": np.ascontiguousarray(np.asarray(w_out, f32)[0]),
        "w_pg": np.ascontiguousarray(np.asarray(w_ple_gate, f32)[0]),
        "w_pp": np.ascontiguousarray(np.asarray(w_ple_proj, f32)[0]),
        "gains": gains, "bfg": bfg, "rb_toep": rb_toep, "amask": am, "cmask": cmask, "cmat": cm, "sel8c": sel8c,
    }
    in_maps = []
    zeros_prev = np.zeros((NTOK, D), f32)
    for c in range(8):
        b, half = c // 2, c % 2
        m = dict(shared)
        m["x_own"] = np.ascontiguousarray(x[b, half * NTOK:(half + 1) * NTOK])
        m["x_prev"] = np.ascontiguousarray(x[b, 0:NTOK]) if half == 1 else zeros_prev
        m["p_own"] = np.ascontiguousarray(p[0, b, half * NTOK:(half + 1) * NTOK])
        m["mprev"] = np.full((128, 1), 0.0 if half == 1 else NEG, f32)
        in_maps.append(m)
    return in_maps


def kernel(**inputs):
    f32 = np.float32
    if "nc" not in _NC_CACHE:
        _NC_CACHE["nc"] = build()
    nc = _NC_CACHE["nc"]
    in_maps = make_in_maps(**inputs)
    res = run_bass_kernel_spmd(nc, in_maps, core_ids=list(range(8)))
    outp = np.empty((4, 2 * NTOK, D), f32)
    for c in range(8):
        b, half = c // 2, c % 2
        outp[b, half * NTOK:(half + 1) * NTOK] = res.results[c]["out"]
    return outp
```
